# Optimizing a Trainium2 kernel written in Bass

```python
import jax, jax.numpy as jnp
from jax import lax
import numpy as np

D_MODEL = 1024
BATCH = 8
SEQ = 4096
DEPTH = 2

HEAD_DIM = 64
D_MIX = D_MODEL
A_Q_HEADS = D_MIX // (2 * HEAD_DIM)
A_KV_HEADS = max(1, A_Q_HEADS // 4)
B_HEADS = D_MIX // (2 * HEAD_DIM)
A_WIDTH = A_Q_HEADS * HEAD_DIM
A_KV_WIDTH = A_KV_HEADS * HEAD_DIM
B_WIDTH = B_HEADS * HEAD_DIM
A_HALF_WINDOW = 128
B_BRANCHES = ((128, 1), (512, 4), (2048, 16))
ROPE_THETA = 500000.0
ROPE_DIM = HEAD_DIM // 4
D_FF = int(round(8 * D_MODEL / 3 / 256)) * 256
N_MOD = 9
IN_SIZES = (A_WIDTH, A_KV_WIDTH, A_KV_WIDTH, B_WIDTH, B_WIDTH, B_WIDTH)
IN_SPLITS = tuple(int(v) for v in np.cumsum(IN_SIZES)[:-1])
D_IN = int(sum(IN_SIZES))
EPS = 1e-6
NEG_INF = -1e30

kernel_name = "hybrid_parallel_swa_dilated_macaron_adaln"


def rmsnorm(x, g):
    xf = x.astype(jnp.float32)
    y = xf * lax.rsqrt(jnp.mean(xf * xf, axis=-1, keepdims=True) + EPS)
    return (y * g.astype(jnp.float32)).astype(x.dtype)


def modulate(h, shift, scale):
    return h * (1 + scale) + shift


def swiglu(h, wi, wo):
    gate, up = jnp.split(h @ wi, 2, axis=-1)
    return (jax.nn.silu(gate) * up) @ wo


def apply_rope(t, cos, sin):
    half = ROPE_DIM // 2
    tr = t[..., :ROPE_DIM].astype(jnp.float32)
    t1, t2 = tr[..., :half], tr[..., half:]
    rot = jnp.concatenate([t1 * cos - t2 * sin, t2 * cos + t1 * sin], axis=-1)
    return jnp.concatenate([rot.astype(t.dtype), t[..., ROPE_DIM:]], axis=-1)


def banded_attention(q, k, v, half_window, sink=None):
    n, L, hq, dh = q.shape
    hkv = k.shape[2]
    g = hq // hkv
    w = half_window
    nb = -(-L // w)
    pad = nb * w - L
    qb = jnp.pad(q * (dh ** -0.5), ((0, 0), (0, pad), (0, 0), (0, 0))).reshape(n, nb, w, hkv, g, dh)
    kp = jnp.pad(k, ((0, 0), (w, w + pad), (0, 0), (0, 0)))
    vp = jnp.pad(v, ((0, 0), (w, w + pad), (0, 0), (0, 0)))
    kidx = (jnp.arange(nb) * w)[:, None] + jnp.arange(3 * w)[None, :]
    kb = kp[:, kidx]
    vb = vp[:, kidx]
    s = jnp.einsum("nbqhgd,nbkhd->nbhgqk", qb, kb, preferred_element_type=jnp.float32)
    qpos = (jnp.arange(nb) * w)[:, None] + jnp.arange(w)[None, :]
    kpos = kidx - w
    valid = ((jnp.abs(qpos[:, :, None] - kpos[:, None, :]) <= w)
             & (kpos[:, None, :] >= 0) & (kpos[:, None, :] < L))
    s = jnp.where(valid[None, :, None, None], s, NEG_INF)
    lse = jax.nn.logsumexp(s, axis=-1)
    if sink is None:
        denom = lse
    else:
        denom = jnp.logaddexp(lse, sink.astype(jnp.float32).reshape(hkv, g)[None, None, :, :, None])
    p = jnp.exp(s - denom[..., None]).astype(v.dtype)
    o = jnp.einsum("nbhgqk,nbkhd->nbqhgd", p, vb).reshape(n, nb * w, hq, dh)[:, :L]
    lse = lse.transpose(0, 1, 4, 2, 3).reshape(n, nb * w, hq)[:, :L]
    return o, lse


def dilated_branch(q, k, v, window, dilation):
    b, s, h, dh = q.shape
    L = s // dilation

    def to_sub(t):
        return t.reshape(b, L, dilation, h, dh).transpose(0, 2, 1, 3, 4).reshape(b * dilation, L, h, dh)

    o, lse = banded_attention(to_sub(q), to_sub(k), to_sub(v), window // (2 * dilation))
    o = o.reshape(b, dilation, L, h, dh).transpose(0, 2, 1, 3, 4).reshape(b, s, h, dh)
    lse = lse.reshape(b, dilation, L, h).transpose(0, 2, 1, 3).reshape(b, s, h)
    return o, lse


def token_mixing(h, cos, sin, w_in, sink, onorm_a, onorm_b, w_out):
    b, s, _ = h.shape
    proj = h @ w_in
    qa, ka, va, qb, kb, vb = jnp.split(proj, IN_SPLITS, axis=-1)
    qa = apply_rope(qa.reshape(b, s, A_Q_HEADS, HEAD_DIM), cos, sin)
    ka = apply_rope(ka.reshape(b, s, A_KV_HEADS, HEAD_DIM), cos, sin)
    va = va.reshape(b, s, A_KV_HEADS, HEAD_DIM)
    qb = apply_rope(qb.reshape(b, s, B_HEADS, HEAD_DIM), cos, sin)
    kb = apply_rope(kb.reshape(b, s, B_HEADS, HEAD_DIM), cos, sin)
    vb = vb.reshape(b, s, B_HEADS, HEAD_DIM)
    oa, _ = banded_attention(qa, ka, va, A_HALF_WINDOW, sink)
    outs, lses = [], []
    for window, dilation in B_BRANCHES:
        o, l = dilated_branch(qb, kb, vb, window, dilation)
        outs.append(o)
        lses.append(l)
    wts = jax.nn.softmax(jnp.stack(lses, axis=0), axis=0)
    ob = jnp.sum(wts[..., None] * jnp.stack(outs, axis=0).astype(jnp.float32), axis=0).astype(h.dtype)
    ya = rmsnorm(oa.reshape(b, s, A_WIDTH), onorm_a)
    yb = rmsnorm(ob.reshape(b, s, B_WIDTH), onorm_b)
    return jnp.concatenate([ya, yb], axis=-1) @ w_out


def setup_inputs(seed: int = 0) -> dict:
    key = jax.random.key(seed)
    ks = jax.random.split(key, 20)
    f32 = jnp.float32

    def nrm(k, shape, scale):
        return jax.random.normal(k, shape, f32) * scale

    def gain(k, shape):
        return 1.0 + 0.02 * jax.random.normal(k, shape, f32)

    x = nrm(ks[0], (BATCH, SEQ, D_MODEL), 1.0)
    c = nrm(ks[1], (BATCH, D_MODEL), 1.0)
    positions = (jnp.arange(SEQ, dtype=jnp.int32)[None, :]
                 + jax.random.randint(ks[2], (BATCH, 1), 0, 1024, dtype=jnp.int32))
    return {
        "x": x,
        "c": c,
        "positions": positions,
        "ada_w": nrm(ks[3], (DEPTH, D_MODEL, N_MOD * D_MODEL), 0.5 * D_MODEL ** -0.5),
        "ada_b": nrm(ks[4], (DEPTH, N_MOD * D_MODEL), 0.02),
        "norm_ffn1": gain(ks[5], (DEPTH, D_MODEL)),
        "ffn1_wi": nrm(ks[6], (DEPTH, D_MODEL, 2 * D_FF), D_MODEL ** -0.5),
        "ffn1_wo": nrm(ks[7], (DEPTH, D_FF, D_MODEL), D_FF ** -0.5),
        "norm_mix": gain(ks[8], (DEPTH, D_MODEL)),
        "w_in": nrm(ks[9], (DEPTH, D_MODEL, D_IN), D_MODEL ** -0.5),
        "sink": nrm(ks[10], (DEPTH, A_Q_HEADS), 0.5),
        "onorm_a": gain(ks[11], (DEPTH, A_WIDTH)),
        "onorm_b": gain(ks[12], (DEPTH, B_WIDTH)),
        "w_out": nrm(ks[13], (DEPTH, D_MIX, D_MODEL), D_MIX ** -0.5),
        "norm_ffn2": gain(ks[14], (DEPTH, D_MODEL)),
        "ffn2_wi": nrm(ks[15], (DEPTH, D_MODEL, 2 * D_FF), D_MODEL ** -0.5),
        "ffn2_wo": nrm(ks[16], (DEPTH, D_FF, D_MODEL), D_FF ** -0.5),
        "final_norm": gain(ks[17], (D_MODEL,)),
    }


def reference(x, c, positions, ada_w, ada_b, norm_ffn1, ffn1_wi, ffn1_wo, norm_mix, w_in, sink,
              onorm_a, onorm_b, w_out, norm_ffn2, ffn2_wi, ffn2_wo, final_norm):
    b = x.shape[0]
    inv_freq = ROPE_THETA ** (-jnp.arange(0, ROPE_DIM, 2, dtype=jnp.float32) / ROPE_DIM)
    ang = positions.astype(jnp.float32)[:, :, None, None] * inv_freq
    cos, sin = jnp.cos(ang), jnp.sin(ang)
    c_act = jax.nn.silu(c)
    for l in range(DEPTH):
        mod = (c_act @ ada_w[l] + ada_b[l]).reshape(b, N_MOD, 1, D_MODEL)
        sh1, sc1, g1 = mod[:, 0], mod[:, 1], mod[:, 2]
        sh2, sc2, g2 = mod[:, 3], mod[:, 4], mod[:, 5]
        sh3, sc3, g3 = mod[:, 6], mod[:, 7], mod[:, 8]
        h = modulate(rmsnorm(x, norm_ffn1[l]), sh1, sc1)
        x = x + 0.5 * g1 * swiglu(h, ffn1_wi[l], ffn1_wo[l])
        h = modulate(rmsnorm(x, norm_mix[l]), sh2, sc2)
        x = x + g2 * token_mixing(h, cos, sin, w_in[l], sink[l], onorm_a[l], onorm_b[l], w_out[l])
        h = modulate(rmsnorm(x, norm_ffn2[l]), sh3, sc3)
        x = x + 0.5 * g3 * swiglu(h, ffn2_wi[l], ffn2_wo[l])
    return rmsnorm(x, final_norm)
```

```python
import contextlib
import numpy as np
import ml_dtypes
import concourse.bass as bass
import concourse.mybir as mybir
from concourse.bass_utils import run_bass_kernel_spmd

F32 = mybir.dt.float32
BF16 = mybir.dt.bfloat16
I32 = mybir.dt.int32
ALU = mybir.AluOpType
AF = mybir.ActivationFunctionType

S = 4096
D = 1024
DFF = 2816
DIN = 2304
NL = 2
NK = 8
NF = 22
EPS = 1e-6
import os
DBG = os.environ.get('KDBG', '')
TT = 1024
NSUB = TT // 512
NT = S // TT
ENGS = ("pe", "act", "dve", "pool", "sp")


class Buf:
    __slots__ = ("name", "w", "r", "pre")

    def __init__(self, name=""):
        self.name = name
        self.w = []
        self.r = {}
        self.pre = {}


class Prog:
    def __init__(self, nc):
        self.nc = nc
        self.ops = {e: [] for e in ENGS}
        self.cnt = {}
        self.waited = {e: {} for e in ENGS}
        self.semnames = []
        self.esem = {}
        for e in ("pe", "act", "dve", "pool"):
            self.esem[e] = self.new_sem("e_" + e)
        self.ninst = 0

    def new_sem(self, name):
        assert name not in self.cnt
        self.cnt[name] = 0
        self.semnames.append(name)
        return name

    def emit(self, eng, fn, reads=(), writes=(), wadd=(), inc=True, sem=None, incv=1):
        need = {}
        for b in reads:
            for (s, v) in b.w:
                if need.get(s, 0) < v:
                    need[s] = v
        for b in writes:
            for (s, v) in b.w:
                if need.get(s, 0) < v:
                    need[s] = v
            for s, v in b.r.items():
                if need.get(s, 0) < v:
                    need[s] = v
        for b in wadd:
            for s, v in b.r.items():
                if need.get(s, 0) < v:
                    need[s] = v
            for s, v in b.pre.items():
                if need.get(s, 0) < v:
                    need[s] = v
        wd = self.waited[eng]
        for s, v in need.items():
            if wd.get(s, 0) >= v:
                continue
            if eng == "pe" and s == self.esem["pe"]:
                continue
            self.ops[eng].append(("wait", s, v))
            wd[s] = v
        s = sem if sem is not None else self.esem[eng]
        if inc:
            self.cnt[s] += incv
            ev = (s, self.cnt[s])
        else:
            ev = (s, self.cnt[s] + incv)
        self.ops[eng].append(("inst", fn, s if inc else None, incv))
        self.ninst += 1
        for b in reads:
            if b.r.get(s, 0) < ev[1]:
                b.r[s] = ev[1]
        for b in writes:
            pre = dict(b.r)
            for (s2, v2) in b.w:
                if pre.get(s2, 0) < v2:
                    pre[s2] = v2
            b.pre = pre
            b.w = [ev]
            b.r = {}
        for b in wadd:
            b.w.append(ev)
        return ev

    def dma(self, q, sem, out, in_, reads=(), writes=(), wadd=(), **kw):
        return self.emit(q, lambda e: e.dma_start(out=out, in_=in_, **kw), reads=reads,
                         writes=writes, wadd=wadd, sem=sem, incv=16)

    def barrier(self, engs=ENGS, skip=None):
        for e in engs:
            wd = self.waited[e]
            for s in self.semnames:
                if skip and s.startswith(skip):
                    continue
                v = self.cnt[s]
                if v > wd.get(s, 0):
                    if e == "pe" and s == self.esem["pe"]:
                        continue
                    self.ops[e].append(("wait", s, v))
                    wd[s] = v

    def finish(self):
        nc = self.nc
        with contextlib.ExitStack() as st:
            sems = {n: st.enter_context(nc.semaphore(n)) for n in self.semnames}
            block = st.enter_context(nc.Block())
            ops = self.ops

            def replay(engine, lst):
                for op in lst:
                    if op[0] == "wait":
                        engine.wait_ge(sems[op[1]], op[2])
                    else:
                        ins = op[1](engine)
                        if op[2] is not None:
                            ins.then_inc(sems[op[2]], op[3])

            @block.tensor
            def _(e):
                replay(e, ops["pe"])

            @block.scalar
            def _(e):
                replay(e, ops["act"])

            @block.vector
            def _(e):
                replay(e, ops["dve"])

            @block.gpsimd
            def _(e):
                replay(e, ops["pool"])

            @block.sync
            def _(e):
                replay(e, ops["sp"])


def _consts():
    ki = np.arange(128)[:, None]
    qa = np.arange(384)[None, :]
    qb = np.arange(256)[None, :]
    maskA = (np.abs(qa - 128 - ki) <= 128).astype(np.float32)
    maskB = (np.abs(qb - 64 - ki) <= 64).astype(np.float32)
    cmask = np.concatenate([maskA, maskB], axis=1).astype(ml_dtypes.bfloat16)
    perm = np.zeros((128, 128), np.float32)
    inv_freq = (np.float32(500000.0) ** (-np.arange(0, 16, 2, dtype=np.float32) / np.float32(16))).astype(np.float32)
    rc = np.zeros((128, 2), np.float32)
    for hb in (0, 64):
        for i in range(8):
            perm[hb + i + 8, hb + i] = 1.0
            perm[hb + i, hb + i + 8] = 1.0
            rc[hb + i, 0] = inv_freq[i]
            rc[hb + i + 8, 0] = inv_freq[i]
            rc[hb + i, 1] = -1.0
            rc[hb + i + 8, 1] = 1.0
    return dict(identf=np.eye(128, dtype=np.float32), cmask=cmask,
                permm=perm.astype(ml_dtypes.bfloat16), rc=rc)


def build(stop_after=None, debug=False):
    nc = bass.Bass("TRN2", target_bir_lowering=False)
    P = Prog(nc)
    skind = "ExternalOutput" if debug else "Internal"

    def din(name, shape, dt=F32):
        return nc.dram_tensor(name, list(shape), dt, kind="ExternalInput").ap()

    def dscr(name, shape, dt, dbg=False):
        return nc.dram_tensor(name, list(shape), dt, kind=(skind if dbg else "Internal")).ap()

    x_in = din("x", [S, D])
    cT_in = din("cT", [128, 8])
    pos_in = din("pos", [1, S], I32)
    adaw_in = din("ada_w", [NL, D, 9 * D])
    adab_in = din("ada_bT", [NL, 128, 72])
    gam_in = din("gam", [128, 56])
    onorm_in = din("onorm", [NL, 1024])
    sink_in = din("sink", [1, 16])
    w_in_ = {
        "f1wi": din("ffn1_wi", [NL, D, 2 * DFF]), "f1wo": din("ffn1_wo", [NL, DFF, D]),
        "win": din("w_in", [NL, D, DIN]), "wout": din("w_out", [NL, D, D]),
        "f2wi": din("ffn2_wi", [NL, D, 2 * DFF]), "f2wo": din("ffn2_wo", [NL, DFF, D]),
    }
    identf_in = din("identf", [128, 128])
    cmask_in = din("cmask", [128, 640], BF16)
    permm_in = din("permm", [128, 128], BF16)
    rc_in = din("rc", [128, 2])
    out = nc.dram_tensor("out", [S, D], F32, kind="ExternalOutput").ap()

    wshape = {"f1wi": [D, 2 * DFF], "f1wo": [DFF, D], "win": [D, DIN], "wout": [D, D],
              "f2wi": [D, 2 * DFF], "f2wo": [DFF, D]}
    wbf = {(n, l): dscr("wbf_%s%d" % (n, l), wshape[n], BF16) for n in wshape for l in range(NL)}
    xs = dscr("xs", [NK, 128, S], F32, dbg=True)
    qk = dscr("qk", [13, 128, S], BF16, dbg=True)
    vs = dscr("vs", [S, 640], BF16, dbg=True)
    oz = dscr("oz", [4, S, 520], F32, dbg=True)
    b_wbf = {k: Buf("wbf") for k in wbf}
    b_adaw = [Buf(), Buf()]
    b_xs = [Buf("xs%d" % i) for i in range(NT)]
    b_qk, b_vs, b_oz = Buf("qk"), Buf("vs"), Buf("oz")

    def sb(name, shape, dt):
        return nc.alloc_sbuf_tensor("sb_" + name, list(shape), dt)

    identf = sb("identf", [128, 128], F32)
    identb = sb("identb", [128, 128], BF16)
    onesb = sb("onesb", [128, 128], BF16)
    permb = sb("permb", [128, 128], BF16)
    cmask = sb("cmask", [128, 640], BF16)
    rc = sb("rc", [128, 2], F32)
    gam = sb("gam", [128, 56], F32)
    mod = sb("mod", [128, 144], F32)
    Asc = sb("Asc", [128, 48], F32)
    Gsc = sb("Gsc", [128, 48], F32)
    cT = sb("cT", [128, 8], F32)
    cactb = sb("cactb", [128, 8], BF16)
    esink = sb("esink", [128, 16], F32)
    gbc = sb("gbc", [128, NL * 1024], F32)
    COS = sb("COS", [128, S], F32)
    SIN = sb("SIN", [128, S], F32)
    ARENA = 164 * 1024
    arena = sb("arena", [128, ARENA // 2], BF16)
    b_const = Buf("const")
    b_mod = Buf("mod")
    b_rope = Buf("rope")

    class Carver:
        def __init__(self):
            self.off = 0

        def reset(self):
            self.off = 0

        def get(self, shape, dt):
            n = int(np.prod(shape[1:]))
            nbytes = n * (4 if dt in (F32, I32) else 2)
            nbytes = (nbytes + 63) // 64 * 64
            assert self.off + nbytes <= ARENA, (self.off, nbytes)
            ap = arena[:, self.off // 2:(self.off + nbytes) // 2]
            if dt != BF16:
                ap = ap.bitcast(dt)
            ap = ap[:, 0:n]
            self.off += nbytes
            if len(shape) == 3:
                ap = ap.rearrange("p (a b) -> p a b", a=shape[1])
            elif len(shape) == 4:
                ap = ap.rearrange("p (a b c) -> p a b c", a=shape[1], b=shape[2])
            return ap

    cv = Carver()

    psb = [nc.alloc_psum_tensor("ps%d" % i, [128, 512], F32) for i in range(8)]
    b_ps = [Buf("ps%d" % i) for i in range(8)]

    nsem = [0]
    sem_free = []
    sem_used = []

    def dsem(tag="d"):
        if sem_free:
            s_ = sem_free.pop()
        else:
            nsem[0] += 1
            s_ = P.new_sem("d%d" % nsem[0])
        sem_used.append(s_)
        return s_

    def phase_end(skip=None):
        P.barrier(skip=skip)
        sem_free.extend(sem_used)
        del sem_used[:]

    def mm(out_, lhsT, rhs, start, stop, reads, writes=(), wadd=(), inc=False):
        P.emit("pe", lambda e: e.matmul(out_, lhsT=lhsT, rhs=rhs, start=start, stop=stop, skip_group_check=True),
               reads=reads, writes=writes, wadd=wadd, inc=inc)

    def mm_group(out_, pairs, reads, bout):
        n = len(pairs)
        for i, (l_, r_) in enumerate(pairs):
            mm(out_, l_, r_, i == 0, i == n - 1, reads, writes=[bout] if i == 0 else (),
               wadd=[bout] if i else (), inc=(i == n - 1))

    def act(fn, reads, writes=(), wadd=()):
        P.emit("act", fn, reads=reads, writes=writes, wadd=wadd)

    def dve(fn, reads, writes=(), wadd=()):
        P.emit("dve", fn, reads=reads, writes=writes, wadd=wadd)

    def pool(fn, reads, writes=(), wadd=()):
        P.emit("pool", fn, reads=reads, writes=writes, wadd=wadd)

    s_c = dsem("c")
    for dst, src in ((identf, identf_in), (cmask, cmask_in), (permb, permm_in), (rc, rc_in),
                     (gam, gam_in), (cT, cT_in)):
        P.dma("sp", s_c, dst[:], src, wadd=[b_const])
    P.dma("sp", s_c, esink[:], sink_in.partition_broadcast(128)[:, 0, :], wadd=[b_const])
    P.dma("sp", s_c, gbc[:], onorm_in.rearrange("l n -> (l n)").partition_broadcast(128), wadd=[b_const])
    s_cast = {}

    def cast_w(dst, src, bdst, rows, tag, after=()):
        s_ = P.new_sem("cw%d" % len(P.semnames))
        r0 = 0
        step = 256
        while r0 < rows:
            r1 = min(rows, r0 + step)
            P.dma("pool", s_, dst[r0:r1, :], src[r0:r1, :], reads=list(after), wadd=[bdst])
            r0 = r1

    pending = {}

    def queue_cast(key):
        n, l = key
        rows = wshape[n][0]
        s_ = P.new_sem("cw_%s%d" % (n, l))
        lst = []
        r0 = 0
        while r0 < rows:
            r1 = min(rows, r0 + (32 if wshape[n][1] > 4096 else 64))
            lst.append((s_, wbf[key][r0:r1, :], w_in_[n][l][r0:r1, :], b_wbf[key]))
            r0 = r1
        pending[key] = lst

    for l_ in range(NL):
        for n_ in ("f1wi", "f1wo", "win", "wout", "f2wi", "f2wo"):
            if not (l_ == 0 and n_ in ("f1wi", "f1wo")):
                queue_cast((n_, l_))

    def pump(keys, frac_num, frac_den, dep):
        for key in keys:
            lst = pending[key]
            n = len(lst)
            a_ = (n * frac_num) // frac_den
            b_ = (n * (frac_num + 1)) // frac_den
            for (s_, dst, src, bdst) in lst[a_:b_]:
                P.dma("pool", s_, dst, src, reads=list(dep), wadd=[bdst])

    WIB = 11
    b_wi0 = [Buf("wi0_%d" % i) for i in range(WIB)]
    for cb in (0, 5, 6, 1, 7, 2, 8, 3, 9, 4, 10):
        s_ = P.new_sem("cwb%d" % cb)
        P.dma("pool", s_, wbf[("f1wi", 0)][:, cb * 512:(cb + 1) * 512], w_in_["f1wi"][0][:, cb * 512:(cb + 1) * 512], wadd=[b_wi0[cb]])
    cast_w(wbf[("f1wo", 0)], w_in_["f1wo"][0], b_wbf[("f1wo", 0)], wshape["f1wo"][0], "f1wo")

    pool(lambda e: e.memset(onesb[:], 1.0), [], writes=[Buf()])
    b_ident = Buf()
    dve(lambda e: e.tensor_copy(identb[:], identf[:]), [b_const], writes=[b_ident])
    act(lambda e: e.activation(esink[:], esink[:], AF.Exp), [b_const], wadd=[b_const])
    b_cact = Buf()
    act(lambda e: e.activation(cactb[:], cT[:], AF.Silu), [b_const], writes=[b_cact])

    modv = mod[:].rearrange("p (l j k) -> p l j k", l=2, j=9)
    Av = Asc[:].rearrange("p (l s k) -> p l s k", l=2, s=3)
    Gv = Gsc[:].rearrange("p (l s k) -> p l s k", l=2, s=3)
    gamv = gam[:].rearrange("p (v k) -> p v k", k=8)
    def mod_scalars(l):
        for s_ in range(3):
            dve(lambda e, l=l, s_=s_: e.scalar_tensor_tensor(Av[:, l, s_, :], modv[:, l, 3 * s_ + 1, :], 1.0,
                                                             gamv[:, l * 3 + s_, :], ALU.add, ALU.mult),
                [b_mod, b_const], wadd=[b_mod])
            dve(lambda e, l=l, s_=s_: e.tensor_scalar(Gv[:, l, s_, :], modv[:, l, 3 * s_ + 2, :],
                                                      (1.0 if s_ == 1 else 0.5), None, ALU.mult),
                [b_mod], wadd=[b_mod])

    def Bv(l, s_, k):
        return modv[:, l, 3 * s_, k:k + 1]

    cv.reset()
    posi = cv.get([128, S], I32)
    ang = cv.get([128, S], F32)
    tq = cv.get([128, S], F32)
    ti = cv.get([128, S], I32)
    b_pos, b_ang, b_tq, b_ti = Buf(), Buf(), Buf(), Buf()
    P.dma("sp", s_c, posi, pos_in.partition_broadcast(128)[:, 0, :], writes=[b_pos])
    dve(lambda e: e.tensor_copy(ang, posi), [b_pos], writes=[b_ang])
    dve(lambda e: e.tensor_scalar(ang, ang, rc[:, 0:1], None, ALU.mult), [b_ang, b_const], writes=[b_ang])
    TWO_PI = float(2 * np.pi)
    C1 = 6.28125
    C2 = float(2 * np.pi - 6.28125)
    for tab, shift in ((SIN, 0.0), (COS, float(np.pi / 2))):
        dve(lambda e, shift=shift: e.tensor_scalar(tq, ang, shift, 1.0 / TWO_PI, ALU.add, ALU.mult), [b_ang], writes=[b_tq])
        dve(lambda e: e.tensor_copy(ti, tq), [b_tq], writes=[b_ti])
        dve(lambda e: e.tensor_copy(tq, ti), [b_ti], writes=[b_tq])
        dve(lambda e, tab=tab: e.scalar_tensor_tensor(tab[:], tq, -C1, ang, ALU.mult, ALU.add), [b_tq, b_ang], writes=[b_rope])
        dve(lambda e, tab=tab: e.scalar_tensor_tensor(tab[:], tq, -C2, tab[:], ALU.mult, ALU.add), [b_tq, b_rope], writes=[b_rope])
        if shift != 0.0:
            dve(lambda e, tab=tab, shift=shift: e.tensor_scalar(tab[:], tab[:], shift, None, ALU.add), [b_rope], writes=[b_rope])
        dve(lambda e, tab=tab: e.tensor_scalar(tab[:], tab[:], 3.1415925, -3.1415925, ALU.min, ALU.max), [b_rope], writes=[b_rope])
        act(lambda e, tab=tab: e.activation(tab[:], tab[:], AF.Sin), [b_rope], writes=[b_rope])
    dve(lambda e: e.tensor_scalar(SIN[:], SIN[:], rc[:, 1:2], None, ALU.mult), [b_rope, b_const], writes=[b_rope])
    xs_v = xs.rearrange("k p t -> p k t")

    def load_xT_xs(xT, bx, t, sem):
        P.dma("sp", sem, xT, xs_v[:, :, t * TT:(t + 1) * TT], reads=[b_xs[t]], writes=[bx])

    def store_xT_xs(xT, bx, t, sem):
        P.dma("pool", sem, xs_v[:, :, t * TT:(t + 1) * TT], xT, reads=[bx], writes=[b_xs[t]])

    def norm_h(xT, bx, hT, bh, Aof, Bof, scr):
        sq, bsq, rs, brs, tmp, btmp, stat, bstat = scr
        for s_ in range(NSUB):
            sl = slice(s_ * 512, (s_ + 1) * 512)
            act(lambda e, sl=sl: e.activation(sq, xT[:, :, sl], AF.Square), [bx], writes=[bsq])
            mm_group(stat[:, :], [(onesb[:], sq[:, k, :]) for k in range(8)], [bsq], bstat)
            act(lambda e: e.activation(rs, stat[:, :], AF.Sqrt, bias=eps_t[:, 0:1], scale=1.0 / D), [bstat], writes=[brs])
            dve(lambda e: e.reciprocal(rs, rs), [brs], writes=[brs])
            for k in range(8):
                kk = k % 2
                dve(lambda e, k=k, kk=kk, sl=sl: e.scalar_tensor_tensor(tmp[kk], xT[:, k, sl], Aof(k), rs, ALU.mult, ALU.mult),
                    [bx, brs, b_mod], writes=[btmp[kk]])
                act(lambda e, k=k, kk=kk, sl=sl: e.activation(hT[:, k, sl], tmp[kk], AF.Identity, bias=Bof(k), scale=1.0),
                    [btmp[kk], b_mod], writes=[bh[s_]] if k == 0 else (), wadd=[bh[s_]] if k else ())

    eps_t = sb("eps_t", [128, 1], F32)
    pool(lambda e: e.memset(eps_t[:], EPS), [], writes=[Buf()])

    def phase_xT():
        xin = [cv.get([128, D], F32) for _ in range(2)]
        stg = [cv.get([128, 8, 128], F32) for _ in range(2)]
        b_xin = [Buf(), Buf()]
        b_stg = [Buf(), Buf()]
        s_xin = [dsem("x"), dsem("x")]
        s_stg = [dsem("x"), dsem("x")]
        nb = S // 128
        P.dma("sp", s_xin[0], xin[0], x_in[0:128, :], writes=[b_xin[0]])
        for b in range(nb):
            if b + 1 < nb:
                P.dma("sp", s_xin[(b + 1) % 2], xin[(b + 1) % 2], x_in[(b + 1) * 128:(b + 2) * 128, :], writes=[b_xin[(b + 1) % 2]])
            xi = xin[b % 2]
            st_ = stg[b % 2]
            for half in range(2):
                pt = psb[(b * 2 + half) % 4]
                bpt = b_ps[(b * 2 + half) % 4]
                for q in range(4):
                    kc = half * 4 + q
                    P.emit("pe", lambda e, pt=pt, q=q, kc=kc, xi=xi: e.transpose(pt[:, q * 128:(q + 1) * 128], xi[:, kc * 128:(kc + 1) * 128], identf[:]),
                           reads=[b_xin[b % 2], b_const], writes=[bpt] if q == 0 else (), wadd=[bpt] if q else (), inc=(q == 3))
                src = pt[:, :].rearrange("p (a b) -> p a b", a=4)
                dst = st_[:, half * 4:(half + 1) * 4, :]
                if half == 0:
                    act(lambda e, dst=dst, src=src: e.copy(dst, src), [bpt], writes=[b_stg[b % 2]])
                else:
                    dve(lambda e, dst=dst, src=src: e.tensor_copy(dst, src), [bpt], wadd=[b_stg[b % 2]])
            t = (b * 128) // TT
            P.dma("pool", s_stg[b % 2], xs_v[:, :, b * 128:(b + 1) * 128], st_, reads=[b_stg[b % 2]], wadd=[b_xs[t]])

    phase_xT()
    NAS = 3
    adaw_f = [cv.get([128, 8, 512], F32) for _ in range(NAS)]
    adaw_sb = [cv.get([128, 8, 512], BF16) for _ in range(NAS)]
    b_adaw_f = [Buf() for _ in range(NAS)]
    b_adaw_sb = [Buf() for _ in range(NAS)]
    s_adaw = [dsem("a") for _ in range(NAS)]
    modp = psb[7]
    b_modp = b_ps[7]
    nblk = 9 * D // 512
    seq = [(0, cb) for cb in range(nblk)]

    def load_adaw(n):
        l, cb = seq[n]
        P.dma("act", s_adaw[n % NAS], adaw_f[n % NAS],
              adaw_in[l].rearrange("(k p) n -> p k n", p=128)[:, :, cb * 512:(cb + 1) * 512],
              writes=[b_adaw_f[n % NAS]])

    for n in range(min(NAS - 1, len(seq))):
        load_adaw(n)
    for n, (l, cb) in enumerate(seq):
        if n + NAS - 1 < len(seq):
            load_adaw(n + NAS - 1)
        sl_ = n % NAS
        if n % 2 == 1:
            act(lambda e, sl_=sl_: e.copy(adaw_sb[sl_], adaw_f[sl_]), [b_adaw_f[sl_]], writes=[b_adaw_sb[sl_]])
        else:
            pool(lambda e, sl_=sl_: e.tensor_copy(adaw_sb[sl_], adaw_f[sl_]), [b_adaw_f[sl_]], writes=[b_adaw_sb[sl_]])
        for jj in range(4):
            col = l * 72 + cb * 4 + jj
            for k in range(8):
                mm(modp[:, col:col + 1], adaw_sb[sl_][:, k, jj * 128:(jj + 1) * 128], cactb[:, k:k + 1],
                   k == 0, k == 7, [b_adaw_sb[sl_], b_cact],
                   writes=[b_modp] if (n == 0 and jj == 0 and k == 0) else (),
                   wadd=() if (n == 0 and jj == 0 and k == 0) else [b_modp],
                   inc=(k == 7 and (jj == 3)))
    b_adab = Buf()
    adab_sb = cv.get([128, 144], F32)
    P.dma("sp", s_c, adab_sb.rearrange("p (l n) -> p l n", l=2), adab_in.rearrange("l p n -> p l n"), writes=[b_adab])
    dve(lambda e: e.tensor_tensor(mod[:, 0:72], modp[:, 0:72], adab_sb[:, 0:72], ALU.add), [b_modp, b_adab], writes=[b_mod])
    mod_scalars(0)

    P.barrier(skip="cw")
    if debug:
        s_dbg = dsem("dbg")
        for nm, t_, shp in (("dbg_mod", mod, [128, 144]), ("dbg_cos", COS, [128, S]), ("dbg_sin", SIN, [128, S]),
                            ("dbg_A", Asc, [128, 48]), ("dbg_G", Gsc, [128, 48])):
            d_ = nc.dram_tensor(nm, shp, F32, kind="ExternalOutput").ap()
            P.dma("sp", s_dbg, d_, t_[:], reads=[b_mod, b_rope], wadd=[Buf()])
        P.barrier(skip="cw")

    phase_end(skip="cw")

    def phase_ffn(l, s_idx):
        wi_bf = wbf[("f1wi" if s_idx == 0 else "f2wi", l)]
        wo_bf = wbf[("f1wo" if s_idx == 0 else "f2wo", l)]
        b_wi = b_wbf[("f1wi" if s_idx == 0 else "f2wi", l)]
        b_wo = b_wbf[("f1wo" if s_idx == 0 else "f2wo", l)]
        cv.reset()
        xT = [cv.get([128, 8, TT], F32) for _ in range(2)]
        hT = cv.get([128, 8, TT], BF16)
        aT = cv.get([128, NF, TT], BF16)
        sq = cv.get([128, 8, 512], BF16)
        rs = [cv.get([128, 512], F32) for _ in range(NSUB)]
        sg = [cv.get([128, 512], F32) for _ in range(2)]
        tmp = sg
        NWI, NWO = 3, 2
        wi_s = [cv.get([128, 8, 256], BF16) for _ in range(NWI)]
        wo_s = [cv.get([128, NF, 128], BF16) for _ in range(NWO)]
        bx = [Buf("xT0"), Buf("xT1")]
        bsq = Buf("sq")
        brs = [Buf() for _ in range(NSUB)]
        bh = [Buf("h%d" % i) for i in range(NSUB)]
        ba = [Buf("a%d" % i) for i in range(NSUB)]
        bsg = [Buf(), Buf()]
        btmp = bsg
        b_wis = [Buf() for _ in range(NWI)]
        b_wos = [Buf() for _ in range(NWO)]
        s_wi = [dsem("wi") for _ in range(NWI)]
        s_wo = [dsem("wo") for _ in range(NWO)]
        s_x = [dsem("x"), dsem("x")]
        s_xst = [dsem("xs"), dsem("xs")]
        stat, bstat = psb[6], b_ps[6]
        wi_v = wi_bf.rearrange("(k p) n -> p k n", p=128)
        wo_v = wo_bf.rearrange("(j p) d -> p j d", p=128)
        n_wi = NT * NF
        n_wo = NT * NK
        st = {"wi": 0, "wo": 0}
        Aof = lambda k: Av[:, l, s_idx, k:k + 1]
        Bof = lambda k: Bv(l, s_idx, k)

        def ld_wi(upto):
            while st["wi"] <= min(upto, n_wi - 1):
                n = st["wi"]
                j = n % NF
                sl_ = n % NWI
                if l == 0 and s_idx == 0:
                    rg, ru = [b_wi0[(j * 128) // 512]], [b_wi0[(DFF + j * 128) // 512]]
                else:
                    rg = ru = [b_wi]
                P.dma("sp", s_wi[sl_], wi_s[sl_][:, :, 0:128], wi_v[:, :, j * 128:(j + 1) * 128], reads=rg, writes=[b_wis[sl_]])
                P.dma("sp", s_wi[sl_], wi_s[sl_][:, :, 128:256], wi_v[:, :, DFF + j * 128:DFF + (j + 1) * 128], reads=ru, wadd=[b_wis[sl_]])
                st["wi"] += 1

        def ld_wo(upto):
            while st["wo"] <= min(upto, n_wo - 1):
                n = st["wo"]
                c = n % NK
                sl_ = n % NWO
                P.dma("sp", s_wo[sl_], wo_s[sl_], wo_v[:, :, c * 128:(c + 1) * 128], reads=[b_wo], writes=[b_wos[sl_]])
                st["wo"] += 1

        def norm_a(t, s_):
            x_, bx_ = xT[t % 2], bx[t % 2]
            sl = slice(s_ * 512, (s_ + 1) * 512)
            act(lambda e: e.activation(sq, x_[:, :, sl], AF.Square), [bx_], writes=[bsq])
            mm_group(stat[:, :], [(onesb[:], sq[:, k, :]) for k in range(8)], [bsq], bstat)
            act(lambda e: e.activation(rs[s_], stat[:, :], AF.Sqrt, bias=eps_t[:, 0:1], scale=1.0 / D), [bstat], writes=[brs[s_]])
            dve(lambda e: e.reciprocal(rs[s_], rs[s_]), [brs[s_]], writes=[brs[s_]])

        def norm_b(t):
            x_, bx_ = xT[t % 2], bx[t % 2]
            for s_ in range(NSUB):
                sl = slice(s_ * 512, (s_ + 1) * 512)
                for k in range(8):
                    kk = k % 2
                    dve(lambda e, k=k, kk=kk, sl=sl, s_=s_: e.scalar_tensor_tensor(tmp[kk], x_[:, k, sl], Aof(k), rs[s_], ALU.mult, ALU.mult),
                        [bx_, brs[s_], b_mod], writes=[btmp[kk]])
                    act(lambda e, k=k, kk=kk, sl=sl: e.activation(hT[:, k, sl], tmp[kk], AF.Identity, bias=Bof(k), scale=1.0),
                        [btmp[kk], b_mod], writes=[bh[s_]] if k == 0 else (), wadd=[bh[s_]] if k else ())

        gi = 0
        load_xT_xs(xT[0], bx[0], 0, s_x[0])
        ld_wi(NWI - 1)
        for s_ in range(NSUB):
            norm_a(0, s_)
        norm_b(0)
        for t in range(NT):
            x_, bx_ = xT[t % 2], bx[t % 2]
            if t + 1 < NT:
                load_xT_xs(xT[(t + 1) % 2], bx[(t + 1) % 2], t + 1, s_x[(t + 1) % 2])
            for j in range(NF):
                n = t * NF + j
                if l == 0 and s_idx == 0:
                    pump([("win", 0), ("wout", 0)], n, NT * NF, [bx_])
                ld_wi(n + NWI - 1)
                if j == NF - 4:
                    ld_wo(t * NK + NWO - 1)
                w = wi_s[n % NWI]
                bw = b_wis[n % NWI]
                for s_ in range(NSUB):
                    sl = slice(s_ * 512, (s_ + 1) * 512)
                    gp, bgp = psb[gi % 2], b_ps[gi % 2]
                    up, bup = psb[2 + gi % 2], b_ps[2 + gi % 2]
                    sgt, bsgt = sg[gi % 2], bsg[gi % 2]
                    gi += 1
                    mm_group(gp[:, :], [(w[:, k, 0:128], hT[:, k, sl]) for k in range(8)], [bw, bh[s_]], bgp)
                    mm_group(up[:, :], [(w[:, k, 128:256], hT[:, k, sl]) for k in range(8)], [bw, bh[s_]], bup)
                    act(lambda e, sgt=sgt, gp=gp: e.activation(sgt, gp[:, :], AF.Silu), [bgp], writes=[bsgt])
                    dve(lambda e, j=j, sl=sl, sgt=sgt, up=up: e.tensor_tensor(aT[:, j, sl], sgt, up[:, :], ALU.mult),
                        [bsgt, bup], writes=[ba[s_]] if j == 0 else (), wadd=[ba[s_]] if j else ())
                if t + 1 < NT:
                    if j == NF - 7:
                        norm_a(t + 1, 0)
                    if j == NF - 3:
                        norm_a(t + 1, 1)
            if t + 1 < NT:
                norm_b(t + 1)
            for c in range(NK):
                n = t * NK + c
                ld_wo(n + NWO - 1)
                w = wo_s[n % NWO]
                bw = b_wos[n % NWO]
                for s_ in range(NSUB):
                    sl = slice(s_ * 512, (s_ + 1) * 512)
                    op_, bop = psb[4 + gi % 2], b_ps[4 + gi % 2]
                    gi += 1
                    mm_group(op_[:, :], [(w[:, j, :], aT[:, j, sl]) for j in range(NF)], [bw, ba[s_]], bop)
                    dve(lambda e, c=c, sl=sl, op_=op_, x_=x_: e.scalar_tensor_tensor(x_[:, c, sl], op_[:, :], Gv[:, l, s_idx, c:c + 1],
                                                                                  x_[:, c, sl], ALU.mult, ALU.add),
                        [bop, b_mod, bx_], wadd=[bx_])
            store_xT_xs(x_, bx_, t, s_xst[t % 2])
        phase_end()

    def phase_mi(l):
        cv.reset()
        xTs = [cv.get([128, 8, TT], F32) for _ in range(2)]
        hTs = [cv.get([128, 8, TT], BF16) for _ in range(2)]
        sq = cv.get([128, 8, 512], BF16)
        rs = cv.get([128, 512], F32)
        tmp = [cv.get([128, 512], F32) for _ in range(2)]
        win = cv.get([128, 8, DIN], BF16)
        tb = [cv.get([128, 512], BF16) for _ in range(2)]
        r1 = [cv.get([128, 512], F32) for _ in range(2)]
        r2 = [cv.get([128, 512], F32) for _ in range(2)]
        qst = [cv.get([128, TT], BF16) for _ in range(2)]
        vst = [cv.get([128, 640], BF16) for _ in range(2)]
        bxs = [Buf(), Buf()]
        bsq, brs, bwin = Buf(), Buf(), Buf()
        bhs = [[Buf() for _ in range(NSUB)] for _ in range(2)]
        btmp = [Buf(), Buf()]
        btb, br1, br2 = [Buf(), Buf()], [Buf(), Buf()], [Buf(), Buf()]
        bqst, bvst = [Buf(), Buf()], [Buf(), Buf()]
        s_xs2 = [dsem("x"), dsem("x")]
        s_w = dsem("w")
        s_q = [dsem("q"), dsem("q")]
        s_v = [dsem("v"), dsem("v")]
        stat, bstat = psb[6], b_ps[6]
        wv = wbf[("win", l)].rearrange("(k p) n -> p k n", p=128)
        bwsrc = b_wbf[("win", l)]
        for c in range(4):
            for hh in range(2):
                h = c + 4 * hh
                P.dma("sp", s_w, win[:, :, c * 128 + hh * 64:c * 128 + hh * 64 + 64], wv[:, :, h * 64:(h + 1) * 64],
                      reads=[bwsrc], wadd=[bwin])
        P.dma("sp", s_w, win[:, :, 512:DIN], wv[:, :, 512:DIN], reads=[bwsrc], wadd=[bwin])
        ccols = [c * 128 for c in range(4)] + [512] + [768 + c * 128 for c in range(4)] + [1280 + c * 128 for c in range(4)]
        gi = 0
        qi = 0
        vi = 0
        load_xT_xs(xTs[0], bxs[0], 0, s_xs2[0])
        for t in range(NT):
            xT, bx = xTs[t % 2], bxs[t % 2]
            if t + 1 < NT:
                load_xT_xs(xTs[(t + 1) % 2], bxs[(t + 1) % 2], t + 1, s_xs2[(t + 1) % 2])
            hT, bh = hTs[t % 2], bhs[t % 2]
            if t == 0:
                norm_h(xT, bx, hT, bh, lambda k: Av[:, l, 1, k:k + 1], lambda k: Bv(l, 1, k),
                       (sq, bsq, rs, brs, tmp, btmp, stat, bstat))
            items = []
            for ch in range(13):
                c0 = ccols[ch]
                q_, bq_ = qst[qi % 2], bqst[qi % 2]
                sq_ = s_q[qi % 2]
                qi += 1
                for s_ in range(NSUB):
                    sl = slice(s_ * 512, (s_ + 1) * 512)
                    tsl = slice(t * TT + s_ * 512, t * TT + (s_ + 1) * 512)
                    tp, btp = psb[gi % 2], b_ps[gi % 2]
                    upp, bupp = psb[2 + gi % 2], b_ps[2 + gi % 2]
                    tb_, btb_ = tb[gi % 2], btb[gi % 2]
                    r1_, br1_ = r1[gi % 2], br1[gi % 2]
                    r2_, br2_ = r2[gi % 2], br2[gi % 2]
                    gi += 1

                    def s1(c0=c0, sl=sl, s_=s_, tp=tp, btp=btp, tb_=tb_, btb_=btb_):
                        mm_group(tp[:, :], [(win[:, k, c0:c0 + 128], hT[:, k, sl]) for k in range(8)], [bwin, bh[s_]], btp)
                        act(lambda e: e.copy(tb_, tp[:, :]), [btp], writes=[btb_])

                    def s2(ch=ch, s_=s_, sl=sl, tsl=tsl, tp=tp, btp=btp, upp=upp, bupp=bupp, tb_=tb_, btb_=btb_,
                           r1_=r1_, br1_=br1_, r2_=r2_, br2_=br2_, q_=q_, bq_=bq_, sq_=sq_, t=t):
                        mm_group(upp[:, :], [(permb[:], tb_)], [btb_, b_const], bupp)
                        dve(lambda e: e.tensor_tensor(r1_, tp[:, :], COS[:, tsl], ALU.mult), [btp, b_rope, btb_], writes=[br1_])
                        dve(lambda e: e.tensor_tensor(r2_, upp[:, :], SIN[:, tsl], ALU.mult), [bupp, b_rope], writes=[br2_])
                        pool(lambda e: e.tensor_tensor(q_[:, sl], r1_, r2_, ALU.add), [br1_, br2_],
                             writes=[bq_] if s_ == 0 else (), wadd=[bq_] if s_ else ())
                        if s_ == NSUB - 1:
                            P.dma("pool", sq_, qk[ch, :, t * TT:(t + 1) * TT], q_, reads=[bq_], wadd=[b_qk])

                    items.append((s1, s2))
            for n in range(len(items) + 1):
                if n < len(items):
                    items[n][0]()
                    if l == 0:
                        pump([("f2wi", 0), ("f2wo", 0)], t * len(items) + n, NT * len(items), [bx])
                if n >= 1:
                    items[n - 1][1]()
                if n == 12 and t + 1 < NT:
                    norm_h(xTs[(t + 1) % 2], bxs[(t + 1) % 2], hTs[(t + 1) % 2], bhs[(t + 1) % 2],
                           lambda k: Av[:, l, 1, k:k + 1], lambda k: Bv(l, 1, k),
                           (sq, bsq, rs, brs, tmp, btmp, stat, bstat))
            for blk in range(0 if 'nov' not in DBG else 99, TT // 128):
                s_ = blk // 4
                bsl = slice(blk * 128, (blk + 1) * 128)
                va, bva = psb[4], b_ps[4]
                vb, bvb = psb[5], b_ps[5]
                v_, bv_ = vst[vi % 2], bvst[vi % 2]
                sv_ = s_v[vi % 2]
                vi += 1
                mm_group(va[:, 0:128], [(hT[:, k, bsl], win[:, k, 640:768]) for k in range(8)], [bwin, bh[s_]], bva)
                mm_group(vb[:, :], [(hT[:, k, bsl], win[:, k, 1792:2304]) for k in range(8)], [bwin, bh[s_]], bvb)
                act(lambda e, v_=v_, va=va: e.copy(v_[:, 0:128], va[:, 0:128]), [bva], writes=[bv_])
                dve(lambda e, v_=v_, vb=vb: e.tensor_copy(v_[:, 128:640], vb[:, :]), [bvb], wadd=[bv_])
                tok0 = t * TT + blk * 128
                P.dma("pool", sv_, vs[tok0:tok0 + 128, :], v_, reads=[bv_], wadd=[b_vs])
        phase_end()

    def phase_att(l):
        cv.reset()
        qkT = [cv.get([128, S], BF16) for _ in range(13)]
        NV = 4
        vt = [cv.get([128, 4, 65], BF16) for _ in range(NV)]
        NPT = 6
        pt = [cv.get([128, 384], BF16) for _ in range(NPT)]
        NOS = 3
        ost = [cv.get([128, 260], F32) for _ in range(NOS)]
        bqk = [Buf() for _ in range(13)]
        bvt = [Buf() for _ in range(NV)]
        bpt = [Buf() for _ in range(NPT)]
        bost = [Buf() for _ in range(NOS)]
        s_qk = dsem("qk")
        s_vt = [dsem("vt") for _ in range(NV)]
        s_os = [dsem("os") for _ in range(NOS)]
        for ch in range(13):
            P.dma("sp", s_qk, qkT[ch], qk[ch], reads=[b_qk], writes=[bqk[ch]])
        for i in range(NV):
            pool(lambda e, i=i: e.memset(vt[i][:, :, 64:65], 1.0), [], writes=[bvt[i]])
        cnt = {"v": 0, "p": 0, "o": 0, "st": 0, "acc": 0}
        ST_BANKS = [0, 1, 2]
        ACC_BANKS = [3, 4, 5, 6]
        items = []
        side_ops = []
        if l == 0 and NL > 1:
            a1 = [cv.get([128, 8, 512], BF16) for _ in range(2)]
            ab1 = cv.get([128, 72], F32)
            ba1 = [Buf(), Buf()]
            bab1 = Buf()
            s_a1 = [dsem("a1"), dsem("a1")]
            s_ab1 = dsem("ab1")
            nblk1 = 9 * D // 512
            a1v = adaw_in[1].rearrange("(k p) n -> p k n", p=128)

            def ld_a1(n):
                P.dma("pool", s_a1[n % 2], a1[n % 2], a1v[:, :, n * 512:(n + 1) * 512], writes=[ba1[n % 2]])

            def mk(n):
                def f():
                    if n == 0:
                        P.dma("sp", s_ab1, ab1, adab_in[1], writes=[bab1])
                    if n + 1 < nblk1:
                        ld_a1(n + 1)
                    for jj in range(4):
                        col = n * 4 + jj
                        for k in range(8):
                            first = (n == 0 and jj == 0 and k == 0)
                            mm(psb[7][:, col:col + 1], a1[n % 2][:, k, jj * 128:(jj + 1) * 128], cactb[:, k:k + 1],
                               k == 0, k == 7, [ba1[n % 2], b_cact], writes=[b_ps[7]] if first else (),
                               wadd=() if first else [b_ps[7]], inc=(k == 7 and jj == 3))
                    if n == nblk1 - 1:
                        dve(lambda e: e.tensor_tensor(mod[:, 72:144], psb[7][:, 0:72], ab1, ALU.add), [b_ps[7], bab1, b_mod], wadd=[b_mod])
                        mod_scalars(1)
                return f

            side_ops.append(lambda: ld_a1(0))
            for n in range(nblk1):
                side_ops.append(mk(n))

        def unit(branch, delta, w, qch, kch, pbase, vcol0, nvh, heads_v, ozcol0):
            L = S // delta
            nt = L // 128
            off = w % 128
            maskc0 = 0 if w == 128 else 384
            nh = len(qch)

            def blk_range(m):
                return max(0, 128 * m - off), min(L, 128 * m - off + 128)

            def contrib_tiles(m):
                a, b = blk_range(m)
                return [i for i in range(nt) if (128 * i - w) < b and (128 * i + 128 + w) > a]

            for r in range(delta):
                accs = {}
                for i in range(nt):
                    tile_state = {}
                    qs = max(0, 128 * i - w)
                    qe = min(L, 128 * i + 128 + w)
                    N = qe - qs
                    mc0 = maskc0 + (qs - (128 * i - w))
                    segs = []
                    p_ = qs
                    while p_ < qe:
                        m = (p_ + off) // 128
                        a, b = blk_range(m)
                        e_ = min(qe, b)
                        segs.append((m, p_ - qs, e_ - p_, p_ - a))
                        p_ = e_
                    for hl in range(nh):
                        it = {}

                        def s1(hl=hl, i=i, r=r, tile_state=tile_state, it=it, qs=qs, qe=qe, N=N):
                            if hl == 0:
                                vslot = cnt["v"] % NV
                                cnt["v"] += 1
                                tile_state["v"] = vslot
                                rows = vs[r + delta * 128 * i: r + delta * (128 * i + 127) + 1: delta, vcol0:vcol0 + 64 * nvh]
                                P.dma("sp", s_vt[vslot], vt[vslot][:, 0:nvh, 0:64], rows.rearrange("p (h d) -> p h d", h=nvh),
                                      reads=[b_vs], wadd=[bvt[vslot]])
                            pb = pbase[hl]
                            stb = ST_BANKS[cnt["st"] % len(ST_BANKS)]
                            cnt["st"] += 1
                            it["stb"] = stb
                            kT = qkT[kch[hl]][pb:pb + 64, r + delta * 128 * i: r + delta * (128 * i + 127) + 1: delta]
                            qT = qkT[qch[hl]][pb:pb + 64, r + delta * qs: r + delta * (qe - 1) + 1: delta]
                            mm_group(psb[stb][:, 0:N], [(kT, qT)], [bqk[kch[hl]], bqk[qch[hl]]], b_ps[stb])

                        def s2(it=it, N=N, mc0=mc0):
                            stb = it["stb"]
                            ps_ = cnt["p"] % NPT
                            cnt["p"] += 1
                            it["ps"] = ps_
                            p_sb, bp_ = pt[ps_], bpt[ps_]
                            act(lambda e: e.activation(p_sb[:, 0:N], psb[stb][:, 0:N], AF.Exp, scale=0.125), [b_ps[stb]], writes=[bp_])
                            (pool if (ps_ % 3 == 2) else dve)(lambda e: e.tensor_tensor(p_sb[:, 0:N], p_sb[:, 0:N], cmask[:, mc0:mc0 + N], ALU.mult),
                                                              [bp_, b_const], writes=[bp_])

                        def s3(hl=hl, i=i, r=r, tile_state=tile_state, it=it, segs=segs, accs=accs):
                            p_sb, bp_ = pt[it["ps"]], bpt[it["ps"]]
                            vslot = tile_state["v"]
                            for si, (m, c_, n_, po) in enumerate(segs):
                                if m not in accs:
                                    accs[m] = ACC_BANKS[cnt["acc"] % 4]
                                    cnt["acc"] += 1
                                bank = accs[m]
                                ts = contrib_tiles(m)
                                first = (i == ts[0]) and hl == 0
                                last = (i == ts[-1])
                                o_ap = psb[bank][po:po + n_, hl * 65:(hl + 1) * 65]
                                mm(o_ap, p_sb[:, c_:c_ + n_], vt[vslot][:, heads_v[hl], :], first, last,
                                   [bp_, bvt[vslot]], writes=[b_ps[bank]] if first else (),
                                   wadd=() if first else [b_ps[bank]], inc=(si == len(segs) - 1))
                            if hl != nh - 1:
                                return
                            for m in sorted(list(accs.keys())):
                                ts = contrib_tiles(m)
                                if ts[-1] != i:
                                    continue
                                bank = accs.pop(m)
                                a, b = blk_range(m)
                                n_ = b - a
                                os_ = cnt["o"] % NOS
                                cnt["o"] += 1
                                dve(lambda e, os_=os_, bank=bank, n_=n_: e.tensor_copy(ost[os_][0:n_, 0:nh * 65], psb[bank][0:n_, 0:nh * 65]),
                                    [b_ps[bank]], writes=[bost[os_]])
                                dst = oz[branch, r + delta * a: r + delta * (b - 1) + 1: delta, ozcol0:ozcol0 + nh * 65]
                                P.dma("pool", s_os[os_], dst, ost[os_][0:n_, 0:nh * 65], reads=[bost[os_]], wadd=[b_oz])

                        items.append((s1, s2, s3))

        for g in range(2):
            unit(0, 1, 128, [0, 1, 2, 3], [4, 4, 4, 4], [64 * g] * 4, 64 * g, 1, [0, 0, 0, 0], g * 260)
        for bi, delta in ((1, 1), (2, 4), (3, 16)):
            for hg in range(2):
                qch = [5 + 2 * hg + (hl // 2) for hl in range(4)]
                kch = [9 + 2 * hg + (hl // 2) for hl in range(4)]
                pbs = [64 * (hl % 2) for hl in range(4)]
                unit(bi, delta, 64, qch, kch, pbs, 128 + hg * 256, 4, [0, 1, 2, 3], hg * 260)
        DSK = 5
        every = max(1, (len(items) - 40) // max(1, len(side_ops))) if side_ops else 0
        so = 0
        if NL > 1:
            ckeys = [("f1wi", 1), ("f1wo", 1), ("win", 1), ("wout", 1)] if l == 0 else ([("f2wi", 1), ("f2wo", 1)] if l == 1 else [])
        else:
            ckeys = []
        for n in range(len(items) + DSK):
            if n < len(items):
                items[n][0]()
                items[n][1]()
                if ckeys:
                    pump(ckeys, n, len(items), [bqk[0]])
            if n - DSK >= 0:
                items[n - DSK][2]()
            if side_ops and so < len(side_ops) and n % every == every - 1:
                side_ops[so]()
                so += 1
        while so < len(side_ops):
            side_ops[so]()
            so += 1
        phase_end()

    def phase_cmb(l):
        cv.reset()
        TC = 512
        NG = S // TC
        xT = [cv.get([128, 8, TC], F32) for _ in range(2)]
        wout = cv.get([128, 8, D], BF16)
        yT = [cv.get([128, 8, TC], BF16) for _ in range(2)]
        ozA = [cv.get([128, 4, 520], F32) for _ in range(2)]
        ozB = [cv.get([128, 4, 520], F32) for _ in range(2)]
        ozB4 = cv.get([128, 4, 520], F32)
        ozB16 = cv.get([128, 4, 520], F32)
        on = [cv.get([128, 4, 512], F32) for _ in range(2)]
        y = [cv.get([128, 4, 1024], BF16) for _ in range(2)]
        den = cv.get([128, 2, 4, 8], F32)
        ss = cv.get([128, 2, 4], F32)
        junk = cv.get([128, 4, 512], F32)
        tmpo = [cv.get([128, 512], F32) for _ in range(2)]
        btmpo = [Buf(), Buf()]
        bwout, bden, bss, bjunk = Buf(), Buf(), Buf(), Buf()
        bx = [Buf(), Buf()]
        byT = [Buf(), Buf()]
        by = [[Buf() for _ in range(4)] for _ in range(2)]
        bozA = [Buf(), Buf()]
        bozB = [Buf(), Buf()]
        bozB4, bozB16 = Buf(), Buf()
        bon = [Buf(), Buf()]
        s_w = dsem("w")
        s_x = [dsem("x"), dsem("x")]
        s_xst = [dsem("xs"), dsem("xs")]
        s_ozA = [dsem("oz"), dsem("oz")]
        s_ozB = [dsem("oz"), dsem("oz")]
        s_ozB4, s_ozB16 = dsem("oz"), dsem("oz")
        P.dma("sp", s_w, wout, wbf[("wout", l)].rearrange("(k p) n -> p k n", p=128), reads=[b_wbf[("wout", l)]], writes=[bwout])
        tpsb = psb[7][:, :].bitcast(BF16)
        btps = b_ps[7]
        gbv = gbc[:].rearrange("p (l n) -> p l n", l=NL)
        st = {"gi": 0}

        def load_x(g):
            t = (g * TC) // TT
            P.dma("sp", s_x[g % 2], xT[g % 2], xs_v[:, :, g * TC:(g + 1) * TC], reads=[b_xs[t]], writes=[bx[g % 2]])

        def ozsrc(br, g):
            return oz[br, g * TC:(g + 1) * TC, :].rearrange("(b p) c -> p b c", p=128)

        def load_oz(g):
            P.dma("sp", s_ozA[g % 2], ozA[g % 2], ozsrc(0, g), reads=[b_oz], writes=[bozA[g % 2]])
            P.dma("sp", s_ozB[g % 2], ozB[g % 2], ozsrc(1, g), reads=[b_oz], writes=[bozB[g % 2]])
            P.dma("sp", s_ozB4, ozB4, ozsrc(2, g), reads=[b_oz], writes=[bozB4])
            P.dma("sp", s_ozB16, ozB16, ozsrc(3, g), reads=[b_oz], writes=[bozB16])

        def chain_ops(g):
            oA, bA = ozA[g % 2], bozA[g % 2]
            oB, bB = ozB[g % 2], bozB[g % 2]
            y_, by_ = y[g % 2], by[g % 2]
            oAv = oA.rearrange("p b (h d) -> p b h d", h=8)
            oBv = oB.rearrange("p b (h d) -> p b h d", h=8)
            ops = []
            ops.append(lambda: dve(lambda e: e.tensor_tensor(oB, oB, ozB4, ALU.add), [bB, bozB4], writes=[bB]))
            ops.append(lambda: dve(lambda e: e.tensor_tensor(oB, oB, ozB16, ALU.add), [bB, bozB16], writes=[bB]))
            ops.append(lambda: pool(lambda e: e.memset(ss, 0.0), [], writes=[bss]))
            ops.append(lambda: dve(lambda e: e.tensor_tensor(den[:, 0, :, :], oAv[:, :, :, 64],
                                                             esink[:, l * 8:(l + 1) * 8].unsqueeze(1).broadcast_to([128, 4, 8]), ALU.add),
                                   [bA, b_const], writes=[bden]))
            ops.append(lambda: dve(lambda e: e.tensor_copy(den[:, 1, :, :], oBv[:, :, :, 64]), [bB], wadd=[bden]))
            ops.append(lambda: dve(lambda e: e.reciprocal(den, den), [bden], writes=[bden]))
            for mix, ov, bo in ((0, oAv, bA), (1, oBv, bB)):
                onv = on[mix].rearrange("p b (h d) -> p b h d", h=8)
                ops.append(lambda onv=onv, ov=ov, mix=mix, bo=bo: dve(
                    lambda e: e.tensor_tensor(onv, ov[:, :, :, 0:64], den[:, mix, :, :].unsqueeze(3).broadcast_to([128, 4, 8, 64]), ALU.mult),
                    [bo, bden], writes=[bon[mix]]))

                def sqs(mix=mix):
                    for blk in range(4):
                        act(lambda e, blk=blk: e.activation(junk[:, blk, :], on[mix][:, blk, :], AF.Square, accum_out=ss[:, mix, blk:blk + 1]),
                            [bon[mix]], writes=[bjunk] if blk == 0 else (), wadd=[bss] + ([bjunk] if blk else []))
                ops.append(sqs)
            ops.append(lambda: act(lambda e: e.activation(ss, ss, AF.Sqrt, bias=eps_t[:, 0:1], scale=1.0 / 512), [bss], writes=[bss]))
            ops.append(lambda: dve(lambda e: e.reciprocal(ss, ss), [bss], writes=[bss]))
            for blk in range(4):
                for mix in range(2):
                    ops.append(lambda mix=mix, blk=blk: dve(
                        lambda e: e.scalar_tensor_tensor(y_[:, blk, mix * 512:(mix + 1) * 512], on[mix][:, blk, :], ss[:, mix, blk:blk + 1],
                                                         gbv[:, l, mix * 512:(mix + 1) * 512], ALU.mult, ALU.mult),
                        [bon[mix], bss, b_const], writes=[by_[blk]] if mix == 0 else (), wadd=[by_[blk]] if mix else ()))
                ops.append(lambda blk=blk: transp(g, blk))
            return ops

        def chain(g):
            for op in chain_ops(g):
                op()

        def transp(g, blk):
            y_, by_ = y[g % 2], by[g % 2]
            yT_, byT_ = yT[g % 2], byT[g % 2]
            bank = 6 + (blk % 2)
            tps_ = psb[bank][:, :].bitcast(BF16)
            btp_ = b_ps[bank]
            for fc in range(8):
                P.emit("pe", lambda e, fc=fc: e.transpose(tps_[:, fc * 128:(fc + 1) * 128], y_[:, blk, fc * 128:(fc + 1) * 128], identb[:]),
                       reads=[by_[blk], b_ident], writes=[btp_] if fc == 0 else (), wadd=[btp_] if fc else (), inc=(fc == 7))
            src = tps_.rearrange("p (a b) -> p a b", a=8)
            dst = yT_[:, :, blk * 128:(blk + 1) * 128]
            act(lambda e: e.copy(dst, src), [btp_], writes=[byT_] if blk == 0 else (), wadd=[byT_] if blk else ())

        def wout_chunk(g, c):
            op_, bop = psb[st["gi"] % 2], b_ps[st["gi"] % 2]
            st["gi"] += 1
            x_, bx_ = xT[g % 2], bx[g % 2]
            mm_group(op_[:, :], [(wout[:, fc, c * 128:(c + 1) * 128], yT[g % 2][:, fc, :]) for fc in range(8)], [bwout, byT[g % 2]], bop)
            dve(lambda e: e.scalar_tensor_tensor(x_[:, c, :], op_[:, :], Gv[:, l, 1, c:c + 1], x_[:, c, :], ALU.mult, ALU.add),
                [bop, b_mod, bx_], wadd=[bx_])

        load_x(0)
        load_oz(0)
        chain(0)
        if NG > 1:
            load_oz(1)
        for g in range(NG):
            ops = []
            if g + 1 < NG:
                load_x(g + 1)
                ops = chain_ops(g + 1)
            oi = 0
            for c in range(8):
                wout_chunk(g, c)
                for _ in range(2):
                    if oi < len(ops):
                        ops[oi]()
                        oi += 1
            while oi < len(ops):
                ops[oi]()
                oi += 1
            if g + 2 < NG:
                load_oz(g + 2)
            t = (g * TC) // TT
            first = (g * TC) % TT == 0
            P.dma("pool", s_xst[g % 2], xs_v[:, :, g * TC:(g + 1) * TC], xT[g % 2], reads=[bx[g % 2]],
                  writes=[b_xs[t]] if first else (), wadd=() if first else [b_xs[t]])
        phase_end()

    def phase_final():
        cv.reset()
        TC = 512
        xT = [cv.get([128, 8, TC], F32) for _ in range(2)]
        sq = cv.get([128, 8, 512], BF16)
        rs = cv.get([128, 512], F32)
        yT = cv.get([128, 8, TC], F32)
        ot = [cv.get([128, D], F32) for _ in range(2)]
        bx, bsq, brs, byT = [Buf(), Buf()], Buf(), Buf(), Buf()
        bot = [Buf(), Buf()]
        s_x = [dsem("x"), dsem("x")]
        s_o = [dsem("o"), dsem("o")]
        stat, bstat = psb[6], b_ps[6]
        b_out = Buf()
        ob = 0
        gi = 0
        ntc = S // TC
        P.dma("sp", s_x[0], xT[0], xs_v[:, :, 0:TC], reads=[b_xs[0]], writes=[bx[0]])
        for tc_ in range(ntc):
            if tc_ + 1 < ntc:
                t1 = ((tc_ + 1) * TC) // TT
                P.dma("sp", s_x[(tc_ + 1) % 2], xT[(tc_ + 1) % 2], xs_v[:, :, (tc_ + 1) * TC:(tc_ + 2) * TC], reads=[b_xs[t1]], writes=[bx[(tc_ + 1) % 2]])
            x_, bx_ = xT[tc_ % 2], bx[tc_ % 2]
            act(lambda e, x_=x_: e.activation(sq, x_, AF.Square), [bx_], writes=[bsq])
            mm_group(stat[:, :], [(onesb[:], sq[:, k, :]) for k in range(8)], [bsq], bstat)
            act(lambda e: e.activation(rs, stat[:, :], AF.Sqrt, bias=eps_t[:, 0:1], scale=1.0 / D), [bstat], writes=[brs])
            dve(lambda e: e.reciprocal(rs, rs), [brs], writes=[brs])
            for k in range(8):
                dve(lambda e, k=k, x_=x_: e.scalar_tensor_tensor(yT[:, k, :], x_[:, k, :], gamv[:, 6, k:k + 1], rs, ALU.mult, ALU.mult),
                    [bx_, brs, b_const], writes=[byT] if k == 0 else (), wadd=[byT] if k else ())
            for blk in range(TC // 128):
                o_, bo_ = ot[ob % 2], bot[ob % 2]
                so_ = s_o[ob % 2]
                ob += 1
                for half in range(2):
                    pt_, bpt_ = psb[gi % 4], b_ps[gi % 4]
                    gi += 1
                    for q in range(4):
                        kc = half * 4 + q
                        P.emit("pe", lambda e, pt_=pt_, q=q, kc=kc, blk=blk: e.transpose(pt_[:, q * 128:(q + 1) * 128], yT[:, kc, blk * 128:(blk + 1) * 128], identf[:]),
                               reads=[byT, b_const], writes=[bpt_] if q == 0 else (), wadd=[bpt_] if q else (), inc=(q == 3))
                    if half == 0:
                        act(lambda e, o_=o_, pt_=pt_: e.copy(o_[:, 0:512], pt_[:, :]), [bpt_], writes=[bo_])
                    else:
                        dve(lambda e, o_=o_, pt_=pt_: e.tensor_copy(o_[:, 512:1024], pt_[:, :]), [bpt_], wadd=[bo_])
                tok0 = tc_ * TC + blk * 128
                P.dma("pool", so_, out[tok0:tok0 + 128, :], o_, reads=[bo_], wadd=[b_out])
        phase_end()

    phases = []
    for l in range(NL):
        phases += [("f1_%d" % l, lambda l=l: phase_ffn(l, 0)), ("mi_%d" % l, lambda l=l: phase_mi(l)),
                   ("att_%d" % l, lambda l=l: phase_att(l)), ("cmb_%d" % l, lambda l=l: phase_cmb(l)),
                   ("f2_%d" % l, lambda l=l: phase_ffn(l, 2))]
    phases += [("final", phase_final)]
    for name, fn in phases:
        fn()
        if stop_after == name:
            break
    P.barrier()
    P.finish()
    return nc, P


_CACHE = {}


def _prep_inputs(inputs):
    g = lambda k: np.asarray(inputs[k])
    x, c, positions = g("x"), g("c"), g("positions")
    gam_stack = []
    for l in range(NL):
        gam_stack += [g("norm_ffn1")[l], g("norm_mix")[l], g("norm_ffn2")[l]]
    gam_stack.append(g("final_norm"))
    gam = np.ascontiguousarray(np.stack(gam_stack).reshape(7, 8, 128).transpose(2, 0, 1).reshape(128, 56)).astype(np.float32)
    ada_bT = np.ascontiguousarray(g("ada_b").reshape(NL, 72, 128).transpose(0, 2, 1)).astype(np.float32)
    onorm = np.ascontiguousarray(np.concatenate([g("onorm_a"), g("onorm_b")], axis=1)).astype(np.float32)
    sink = np.ascontiguousarray(g("sink").reshape(1, 16)).astype(np.float32)
    shared = dict(ada_w=np.ascontiguousarray(g("ada_w")), ada_bT=ada_bT, gam=gam, onorm=onorm, sink=sink,
                  ffn1_wi=np.ascontiguousarray(g("ffn1_wi")), ffn1_wo=np.ascontiguousarray(g("ffn1_wo")),
                  w_in=np.ascontiguousarray(g("w_in")), w_out=np.ascontiguousarray(g("w_out")),
                  ffn2_wi=np.ascontiguousarray(g("ffn2_wi")), ffn2_wo=np.ascontiguousarray(g("ffn2_wo")))
    shared.update(_consts())
    maps = []
    for b in range(x.shape[0]):
        m = dict(shared)
        m["x"] = np.ascontiguousarray(x[b])
        m["cT"] = np.ascontiguousarray(c[b].reshape(8, 128).T).astype(np.float32)
        m["pos"] = np.ascontiguousarray(positions[b:b + 1]).astype(np.int32)
        maps.append(m)
    return maps


def kernel(**inputs):
    maps = _prep_inputs(inputs)
    if "nc" not in _CACHE:
        _CACHE["nc"] = build()[0]
    nc = _CACHE["nc"]
    res = run_bass_kernel_spmd(nc, maps, core_ids=list(range(len(maps))))
    return np.stack([np.asarray(r["out"]) for r in res.results], axis=0).astype(np.float32)
```

```python
import contextlib
import numpy as np
import ml_dtypes
import concourse.bass as bass
import concourse.mybir as mybir
from concourse.bass_utils import run_bass_kernel_spmd

F32 = mybir.dt.float32
BF16 = mybir.dt.bfloat16
I32 = mybir.dt.int32
ALU = mybir.AluOpType
AF = mybir.ActivationFunctionType

S = 4096
D = 1024
DFF = 2816
DIN = 2304
NL = 2
NK = 8
NF = 22
EPS = 1e-6
import os
DBG = os.environ.get('KDBG', '')
TT = 1024
NSUB = TT // 512
NT = S // TT
ENGS = ("pe", "act", "dve", "pool", "sp")


class Buf:
    __slots__ = ("name", "w", "r", "pre")

    def __init__(self, name=""):
        self.name = name
        self.w = []
        self.r = {}
        self.pre = {}


class Prog:
    def __init__(self, nc):
        self.nc = nc
        self.ops = {e: [] for e in ENGS}
        self.cnt = {}
        self.waited = {e: {} for e in ENGS}
        self.semnames = []
        self.esem = {}
        for e in ("pe", "act", "dve", "pool"):
            self.esem[e] = self.new_sem("e_" + e)
        self.ninst = 0

    def new_sem(self, name):
        assert name not in self.cnt
        self.cnt[name] = 0
        self.semnames.append(name)
        return name

    def emit(self, eng, fn, reads=(), writes=(), wadd=(), inc=True, sem=None, incv=1):
        need = {}
        for b in reads:
            for (s, v) in b.w:
                if need.get(s, 0) < v:
                    need[s] = v
        for b in writes:
            for (s, v) in b.w:
                if need.get(s, 0) < v:
                    need[s] = v
            for s, v in b.r.items():
                if need.get(s, 0) < v:
                    need[s] = v
        for b in wadd:
            for s, v in b.r.items():
                if need.get(s, 0) < v:
                    need[s] = v
            for s, v in b.pre.items():
                if need.get(s, 0) < v:
                    need[s] = v
        wd = self.waited[eng]
        for s, v in need.items():
            if wd.get(s, 0) >= v:
                continue
            if eng == "pe" and s == self.esem["pe"]:
                continue
            self.ops[eng].append(("wait", s, v))
            wd[s] = v
        s = sem if sem is not None else self.esem[eng]
        if inc:
            self.cnt[s] += incv
            ev = (s, self.cnt[s])
        else:
            ev = (s, self.cnt[s] + incv)
        self.ops[eng].append(("inst", fn, s if inc else None, incv))
        self.ninst += 1
        for b in reads:
            if b.r.get(s, 0) < ev[1]:
                b.r[s] = ev[1]
        for b in writes:
            pre = dict(b.r)
            for (s2, v2) in b.w:
                if pre.get(s2, 0) < v2:
                    pre[s2] = v2
            b.pre = pre
            b.w = [ev]
            b.r = {}
        for b in wadd:
            b.w.append(ev)
        return ev

    def dma(self, q, sem, out, in_, reads=(), writes=(), wadd=(), **kw):
        return self.emit(q, lambda e: e.dma_start(out=out, in_=in_, **kw), reads=reads,
                         writes=writes, wadd=wadd, sem=sem, incv=16)

    def barrier(self, engs=ENGS, skip=None):
        for e in engs:
            wd = self.waited[e]
            for s in self.semnames:
                if skip and s.startswith(skip):
                    continue
                v = self.cnt[s]
                if v > wd.get(s, 0):
                    if e == "pe" and s == self.esem["pe"]:
                        continue
                    self.ops[e].append(("wait", s, v))
                    wd[s] = v

    def finish(self):
        nc = self.nc
        with contextlib.ExitStack() as st:
            sems = {n: st.enter_context(nc.semaphore(n)) for n in self.semnames}
            block = st.enter_context(nc.Block())
            ops = self.ops

            def replay(engine, lst):
                for op in lst:
                    if op[0] == "wait":
                        engine.wait_ge(sems[op[1]], op[2])
                    else:
                        ins = op[1](engine)
                        if op[2] is not None:
                            ins.then_inc(sems[op[2]], op[3])

            @block.tensor
            def _(e):
                replay(e, ops["pe"])

            @block.scalar
            def _(e):
                replay(e, ops["act"])

            @block.vector
            def _(e):
                replay(e, ops["dve"])

            @block.gpsimd
            def _(e):
                replay(e, ops["pool"])

            @block.sync
            def _(e):
                replay(e, ops["sp"])


def _consts():
    ki = np.arange(128)[:, None]
    qa = np.arange(384)[None, :]
    qb = np.arange(256)[None, :]
    maskA = (np.abs(qa - 128 - ki) <= 128).astype(np.float32)
    maskB = (np.abs(qb - 64 - ki) <= 64).astype(np.float32)
    cmask = np.concatenate([maskA, maskB], axis=1).astype(ml_dtypes.bfloat16)
    perm = np.zeros((128, 128), np.float32)
    inv_freq = (np.float32(500000.0) ** (-np.arange(0, 16, 2, dtype=np.float32) / np.float32(16))).astype(np.float32)
    rc = np.zeros((128, 2), np.float32)
    for hb in (0, 64):
        for i in range(8):
            perm[hb + i + 8, hb + i] = 1.0
            perm[hb + i, hb + i + 8] = 1.0
            rc[hb + i, 0] = inv_freq[i]
            rc[hb + i + 8, 0] = inv_freq[i]
            rc[hb + i, 1] = -1.0
            rc[hb + i + 8, 1] = 1.0
    return dict(identf=np.eye(128, dtype=np.float32), cmask=cmask,
                permm=perm.astype(ml_dtypes.bfloat16), rc=rc)


def build(stop_after=None, debug=False):
    nc = bass.Bass("TRN2", target_bir_lowering=False)
    P = Prog(nc)
    skind = "ExternalOutput" if debug else "Internal"

    def din(name, shape, dt=F32):
        return nc.dram_tensor(name, list(shape), dt, kind="ExternalInput").ap()

    def dscr(name, shape, dt, dbg=False):
        return nc.dram_tensor(name, list(shape), dt, kind=(skind if dbg else "Internal")).ap()

    x_in = din("x", [S, D])
    cT_in = din("cT", [128, 8])
    pos_in = din("pos", [1, S], I32)
    adaw_in = din("ada_w", [NL, D, 9 * D])
    adab_in = din("ada_bT", [NL, 128, 72])
    gam_in = din("gam", [128, 56])
    onorm_in = din("onorm", [NL, 1024])
    sink_in = din("sink", [1, 16])
    w_in_ = {
        "f1wi": din("ffn1_wi", [NL, D, 2 * DFF]), "f1wo": din("ffn1_wo", [NL, DFF, D]),
        "win": din("w_in", [NL, D, DIN]), "wout": din("w_out", [NL, D, D]),
        "f2wi": din("ffn2_wi", [NL, D, 2 * DFF]), "f2wo": din("ffn2_wo", [NL, DFF, D]),
    }
    identf_in = din("identf", [128, 128])
    cmask_in = din("cmask", [128, 640], BF16)
    permm_in = din("permm", [128, 128], BF16)
    rc_in = din("rc", [128, 2])
    out = nc.dram_tensor("out", [S, D], F32, kind="ExternalOutput").ap()

    wshape = {"f1wi": [D, 2 * DFF], "f1wo": [DFF, D], "win": [D, DIN], "wout": [D, D],
              "f2wi": [D, 2 * DFF], "f2wo": [DFF, D]}
    wbf = {(n, l): dscr("wbf_%s%d" % (n, l), wshape[n], BF16) for n in wshape for l in range(NL)}
    xs = dscr("xs", [NK, 128, S], F32, dbg=True)
    qk = dscr("qk", [13, 128, S], BF16, dbg=True)
    vs = dscr("vs", [S, 640], BF16, dbg=True)
    oz = dscr("oz", [4, S, 520], F32, dbg=True)
    b_wbf = {k: Buf("wbf") for k in wbf}
    b_adaw = [Buf(), Buf()]
    b_xs = [Buf("xs%d" % i) for i in range(NT)]
    b_qk, b_vs, b_oz = Buf("qk"), Buf("vs"), Buf("oz")

    def sb(name, shape, dt):
        return nc.alloc_sbuf_tensor("sb_" + name, list(shape), dt)

    identf = sb("identf", [128, 128], F32)
    identb = sb("identb", [128, 128], BF16)
    onesb = sb("onesb", [128, 128], BF16)
    permb = sb("permb", [128, 128], BF16)
    cmask = sb("cmask", [128, 640], BF16)
    rc = sb("rc", [128, 2], F32)
    gam = sb("gam", [128, 56], F32)
    mod = sb("mod", [128, 144], F32)
    Asc = sb("Asc", [128, 48], F32)
    Gsc = sb("Gsc", [128, 48], F32)
    cT = sb("cT", [128, 8], F32)
    cactb = sb("cactb", [128, 8], BF16)
    esink = sb("esink", [128, 16], F32)
    gbc = sb("gbc", [128, NL * 1024], F32)
    COS = sb("COS", [128, S], F32)
    SIN = sb("SIN", [128, S], F32)
    ARENA = 164 * 1024
    arena = sb("arena", [128, ARENA // 2], BF16)
    b_const = Buf("const")
    b_mod = Buf("mod")
    b_rope = Buf("rope")

    class Carver:
        def __init__(self):
            self.off = 0

        def reset(self):
            self.off = 0

        def get(self, shape, dt):
            n = int(np.prod(shape[1:]))
            nbytes = n * (4 if dt in (F32, I32) else 2)
            nbytes = (nbytes + 63) // 64 * 64
            assert self.off + nbytes <= ARENA, (self.off, nbytes)
            ap = arena[:, self.off // 2:(self.off + nbytes) // 2]
            if dt != BF16:
                ap = ap.bitcast(dt)
            ap = ap[:, 0:n]
            self.off += nbytes
            if len(shape) == 3:
                ap = ap.rearrange("p (a b) -> p a b", a=shape[1])
            elif len(shape) == 4:
                ap = ap.rearrange("p (a b c) -> p a b c", a=shape[1], b=shape[2])
            return ap

    cv = Carver()

    psb = [nc.alloc_psum_tensor("ps%d" % i, [128, 512], F32) for i in range(8)]
    b_ps = [Buf("ps%d" % i) for i in range(8)]

    nsem = [0]
    sem_free = []
    sem_used = []

    def dsem(tag="d"):
        if sem_free:
            s_ = sem_free.pop()
        else:
            nsem[0] += 1
            s_ = P.new_sem("d%d" % nsem[0])
        sem_used.append(s_)
        return s_

    def phase_end(skip=None):
        P.barrier(skip=skip)
        sem_free.extend(sem_used)
        del sem_used[:]

    def mm(out_, lhsT, rhs, start, stop, reads, writes=(), wadd=(), inc=False):
        P.emit("pe", lambda e: e.matmul(out_, lhsT=lhsT, rhs=rhs, start=start, stop=stop, skip_group_check=True),
               reads=reads, writes=writes, wadd=wadd, inc=inc)

    def mm_group(out_, pairs, reads, bout):
        n = len(pairs)
        for i, (l_, r_) in enumerate(pairs):
            mm(out_, l_, r_, i == 0, i == n - 1, reads, writes=[bout] if i == 0 else (),
               wadd=[bout] if i else (), inc=(i == n - 1))

    def act(fn, reads, writes=(), wadd=()):
        P.emit("act", fn, reads=reads, writes=writes, wadd=wadd)

    def dve(fn, reads, writes=(), wadd=()):
        P.emit("dve", fn, reads=reads, writes=writes, wadd=wadd)

    def pool(fn, reads, writes=(), wadd=()):
        P.emit("pool", fn, reads=reads, writes=writes, wadd=wadd)

    s_c = dsem("c")
    for dst, src in ((identf, identf_in), (cmask, cmask_in), (permb, permm_in), (rc, rc_in),
                     (gam, gam_in), (cT, cT_in)):
        P.dma("sp", s_c, dst[:], src, wadd=[b_const])
    P.dma("sp", s_c, esink[:], sink_in.partition_broadcast(128)[:, 0, :], wadd=[b_const])
    P.dma("sp", s_c, gbc[:], onorm_in.rearrange("l n -> (l n)").partition_broadcast(128), wadd=[b_const])
    s_cast = {}

    def cast_w(dst, src, bdst, rows, tag, after=()):
        s_ = P.new_sem("cw%d" % len(P.semnames))
        r0 = 0
        step = 256
        while r0 < rows:
            r1 = min(rows, r0 + step)
            P.dma("pool", s_, dst[r0:r1, :], src[r0:r1, :], reads=list(after), wadd=[bdst])
            r0 = r1

    pending = {}

    def queue_cast(key):
        n, l = key
        rows = wshape[n][0]
        s_ = P.new_sem("cw_%s%d" % (n, l))
        lst = []
        r0 = 0
        while r0 < rows:
            r1 = min(rows, r0 + (32 if wshape[n][1] > 4096 else 64))
            lst.append((s_, wbf[key][r0:r1, :], w_in_[n][l][r0:r1, :], b_wbf[key]))
            r0 = r1
        pending[key] = lst

    for l_ in range(NL):
        for n_ in ("f1wi", "f1wo", "win", "wout", "f2wi", "f2wo"):
            if not (l_ == 0 and n_ in ("f1wi", "f1wo")):
                queue_cast((n_, l_))

    def pump(keys, frac_num, frac_den, dep):
        for key in keys:
            lst = pending[key]
            n = len(lst)
            a_ = (n * frac_num) // frac_den
            b_ = (n * (frac_num + 1)) // frac_den
            for (s_, dst, src, bdst) in lst[a_:b_]:
                P.dma("pool", s_, dst, src, reads=list(dep), wadd=[bdst])

    WIB = 11
    b_wi0 = [Buf("wi0_%d" % i) for i in range(WIB)]
    for cb in (0, 5, 6, 1, 7, 2, 8, 3, 9, 4, 10):
        s_ = P.new_sem("cwb%d" % cb)
        P.dma("pool", s_, wbf[("f1wi", 0)][:, cb * 512:(cb + 1) * 512], w_in_["f1wi"][0][:, cb * 512:(cb + 1) * 512], wadd=[b_wi0[cb]])
    cast_w(wbf[("f1wo", 0)], w_in_["f1wo"][0], b_wbf[("f1wo", 0)], wshape["f1wo"][0], "f1wo")

    pool(lambda e: e.memset(onesb[:], 1.0), [], writes=[Buf()])
    b_ident = Buf()
    dve(lambda e: e.tensor_copy(identb[:], identf[:]), [b_const], writes=[b_ident])
    act(lambda e: e.activation(esink[:], esink[:], AF.Exp), [b_const], wadd=[b_const])
    b_cact = Buf()
    act(lambda e: e.activation(cactb[:], cT[:], AF.Silu), [b_const], writes=[b_cact])

    modv = mod[:].rearrange("p (l j k) -> p l j k", l=2, j=9)
    Av = Asc[:].rearrange("p (l s k) -> p l s k", l=2, s=3)
    Gv = Gsc[:].rearrange("p (l s k) -> p l s k", l=2, s=3)
    gamv = gam[:].rearrange("p (v k) -> p v k", k=8)
    def mod_scalars(l):
        for s_ in range(3):
            dve(lambda e, l=l, s_=s_: e.scalar_tensor_tensor(Av[:, l, s_, :], modv[:, l, 3 * s_ + 1, :], 1.0,
                                                             gamv[:, l * 3 + s_, :], ALU.add, ALU.mult),
                [b_mod, b_const], wadd=[b_mod])
            dve(lambda e, l=l, s_=s_: e.tensor_scalar(Gv[:, l, s_, :], modv[:, l, 3 * s_ + 2, :],
                                                      (1.0 if s_ == 1 else 0.5), None, ALU.mult),
                [b_mod], wadd=[b_mod])

    def Bv(l, s_, k):
        return modv[:, l, 3 * s_, k:k + 1]

    cv.reset()
    posi = cv.get([128, S], I32)
    ang = cv.get([128, S], F32)
    tq = cv.get([128, S], F32)
    ti = cv.get([128, S], I32)
    b_pos, b_ang, b_tq, b_ti = Buf(), Buf(), Buf(), Buf()
    P.dma("sp", dsem("pos"), posi, pos_in.partition_broadcast(128)[:, 0, :], writes=[b_pos])
    dve(lambda e: e.tensor_copy(ang, posi), [b_pos], writes=[b_ang])
    dve(lambda e: e.tensor_scalar(ang, ang, rc[:, 0:1], None, ALU.mult), [b_ang, b_const], writes=[b_ang])
    TWO_PI = float(2 * np.pi)
    C1 = 6.28125
    C2 = float(2 * np.pi - 6.28125)
    for tab, shift in ((SIN, 0.0), (COS, float(np.pi / 2))):
        dve(lambda e, shift=shift: e.tensor_scalar(tq, ang, shift, 1.0 / TWO_PI, ALU.add, ALU.mult), [b_ang], writes=[b_tq])
        dve(lambda e: e.tensor_copy(ti, tq), [b_tq], writes=[b_ti])
        dve(lambda e: e.tensor_copy(tq, ti), [b_ti], writes=[b_tq])
        dve(lambda e, tab=tab: e.scalar_tensor_tensor(tab[:], tq, -C1, ang, ALU.mult, ALU.add), [b_tq, b_ang], writes=[b_rope])
        dve(lambda e, tab=tab: e.scalar_tensor_tensor(tab[:], tq, -C2, tab[:], ALU.mult, ALU.add), [b_tq, b_rope], writes=[b_rope])
        if shift != 0.0:
            dve(lambda e, tab=tab, shift=shift: e.tensor_scalar(tab[:], tab[:], shift, None, ALU.add), [b_rope], writes=[b_rope])
        dve(lambda e, tab=tab: e.tensor_scalar(tab[:], tab[:], 3.1415925, -3.1415925, ALU.min, ALU.max), [b_rope], writes=[b_rope])
        act(lambda e, tab=tab: e.activation(tab[:], tab[:], AF.Sin), [b_rope], writes=[b_rope])
    dve(lambda e: e.tensor_scalar(SIN[:], SIN[:], rc[:, 1:2], None, ALU.mult), [b_rope, b_const], writes=[b_rope])
    xs_v = xs.rearrange("k p t -> p k t")

    def load_xT_xs(xT, bx, t, sem):
        P.dma("sp", sem, xT, xs_v[:, :, t * TT:(t + 1) * TT], reads=[b_xs[t]], writes=[bx])

    def store_xT_xs(xT, bx, t, sem):
        P.dma("pool", sem, xs_v[:, :, t * TT:(t + 1) * TT], xT, reads=[bx], writes=[b_xs[t]])

    def norm_h(xT, bx, hT, bh, Aof, Bof, scr):
        sq, bsq, rs, brs, tmp, btmp, stat, bstat = scr
        for s_ in range(NSUB):
            sl = slice(s_ * 512, (s_ + 1) * 512)
            act(lambda e, sl=sl: e.activation(sq, xT[:, :, sl], AF.Square), [bx], writes=[bsq])
            mm_group(stat[:, :], [(onesb[:], sq[:, k, :]) for k in range(8)], [bsq], bstat)
            act(lambda e: e.activation(rs, stat[:, :], AF.Sqrt, bias=eps_t[:, 0:1], scale=1.0 / D), [bstat], writes=[brs])
            dve(lambda e: e.reciprocal(rs, rs), [brs], writes=[brs])
            for k in range(8):
                kk = k % 2
                dve(lambda e, k=k, kk=kk, sl=sl: e.scalar_tensor_tensor(tmp[kk], xT[:, k, sl], Aof(k), rs, ALU.mult, ALU.mult),
                    [bx, brs, b_mod], writes=[btmp[kk]])
                act(lambda e, k=k, kk=kk, sl=sl: e.activation(hT[:, k, sl], tmp[kk], AF.Identity, bias=Bof(k), scale=1.0),
                    [btmp[kk], b_mod], writes=[bh[s_]] if k == 0 else (), wadd=[bh[s_]] if k else ())

    eps_t = sb("eps_t", [128, 1], F32)
    pool(lambda e: e.memset(eps_t[:], EPS), [], writes=[Buf()])

    def phase_xT():
        xin = [cv.get([128, D], F32) for _ in range(2)]
        stg = [cv.get([128, 8, 128], F32) for _ in range(2)]
        b_xin = [Buf(), Buf()]
        b_stg = [Buf(), Buf()]
        s_xin = [dsem("x"), dsem("x")]
        s_stg = [dsem("x"), dsem("x")]
        nb = S // 128
        P.dma("sp", s_xin[0], xin[0], x_in[0:128, :], writes=[b_xin[0]])
        for b in range(nb):
            if b + 1 < nb:
                P.dma("sp", s_xin[(b + 1) % 2], xin[(b + 1) % 2], x_in[(b + 1) * 128:(b + 2) * 128, :], writes=[b_xin[(b + 1) % 2]])
            xi = xin[b % 2]
            st_ = stg[b % 2]
            for half in range(2):
                pt = psb[(b * 2 + half) % 4]
                bpt = b_ps[(b * 2 + half) % 4]
                for q in range(4):
                    kc = half * 4 + q
                    P.emit("pe", lambda e, pt=pt, q=q, kc=kc, xi=xi: e.transpose(pt[:, q * 128:(q + 1) * 128], xi[:, kc * 128:(kc + 1) * 128], identf[:]),
                           reads=[b_xin[b % 2], b_const], writes=[bpt] if q == 0 else (), wadd=[bpt] if q else (), inc=(q == 3))
                src = pt[:, :].rearrange("p (a b) -> p a b", a=4)
                dst = st_[:, half * 4:(half + 1) * 4, :]
                if half == 0:
                    act(lambda e, dst=dst, src=src: e.copy(dst, src), [bpt], writes=[b_stg[b % 2]])
                else:
                    dve(lambda e, dst=dst, src=src: e.tensor_copy(dst, src), [bpt], wadd=[b_stg[b % 2]])
            t = (b * 128) // TT
            P.dma("pool", s_stg[b % 2], xs_v[:, :, b * 128:(b + 1) * 128], st_, reads=[b_stg[b % 2]], wadd=[b_xs[t]])

    phase_xT()
    NAS = 3
    adaw_f = [cv.get([128, 8, 512], F32) for _ in range(NAS)]
    adaw_sb = [cv.get([128, 8, 512], BF16) for _ in range(NAS)]
    b_adaw_f = [Buf() for _ in range(NAS)]
    b_adaw_sb = [Buf() for _ in range(NAS)]
    s_adaw = [dsem("a") for _ in range(NAS)]
    modp = psb[7]
    b_modp = b_ps[7]
    nblk = 9 * D // 512
    seq = [(0, cb) for cb in range(nblk)]

    def load_adaw(n):
        l, cb = seq[n]
        P.dma("act", s_adaw[n % NAS], adaw_f[n % NAS],
              adaw_in[l].rearrange("(k p) n -> p k n", p=128)[:, :, cb * 512:(cb + 1) * 512],
              writes=[b_adaw_f[n % NAS]])

    for n in range(min(NAS - 1, len(seq))):
        load_adaw(n)
    for n, (l, cb) in enumerate(seq):
        if n + NAS - 1 < len(seq):
            load_adaw(n + NAS - 1)
        sl_ = n % NAS
        if n % 2 == 1:
            act(lambda e, sl_=sl_: e.copy(adaw_sb[sl_], adaw_f[sl_]), [b_adaw_f[sl_]], writes=[b_adaw_sb[sl_]])
        else:
            pool(lambda e, sl_=sl_: e.tensor_copy(adaw_sb[sl_], adaw_f[sl_]), [b_adaw_f[sl_]], writes=[b_adaw_sb[sl_]])
        for jj in range(4):
            col = l * 72 + cb * 4 + jj
            for k in range(8):
                mm(modp[:, col:col + 1], adaw_sb[sl_][:, k, jj * 128:(jj + 1) * 128], cactb[:, k:k + 1],
                   k == 0, k == 7, [b_adaw_sb[sl_], b_cact],
                   writes=[b_modp] if (n == 0 and jj == 0 and k == 0) else (),
                   wadd=() if (n == 0 and jj == 0 and k == 0) else [b_modp],
                   inc=(k == 7 and (jj == 3)))
    b_adab = Buf()
    adab_sb = cv.get([128, 144], F32)
    P.dma("sp", dsem("ab"), adab_sb.rearrange("p (l n) -> p l n", l=2), adab_in.rearrange("l p n -> p l n"), writes=[b_adab])
    dve(lambda e: e.tensor_tensor(mod[:, 0:72], modp[:, 0:72], adab_sb[:, 0:72], ALU.add), [b_modp, b_adab], writes=[b_mod])
    mod_scalars(0)

    P.barrier(skip="cw")
    if debug:
        s_dbg = dsem("dbg")
        for nm, t_, shp in (("dbg_mod", mod, [128, 144]), ("dbg_cos", COS, [128, S]), ("dbg_sin", SIN, [128, S]),
                            ("dbg_A", Asc, [128, 48]), ("dbg_G", Gsc, [128, 48])):
            d_ = nc.dram_tensor(nm, shp, F32, kind="ExternalOutput").ap()
            P.dma("sp", s_dbg, d_, t_[:], reads=[b_mod, b_rope], wadd=[Buf()])
        P.barrier(skip="cw")

    phase_end(skip="cw")

    def phase_ffn(l, s_idx):
        wi_bf = wbf[("f1wi" if s_idx == 0 else "f2wi", l)]
        wo_bf = wbf[("f1wo" if s_idx == 0 else "f2wo", l)]
        b_wi = b_wbf[("f1wi" if s_idx == 0 else "f2wi", l)]
        b_wo = b_wbf[("f1wo" if s_idx == 0 else "f2wo", l)]
        cv.reset()
        xT = [cv.get([128, 8, TT], F32) for _ in range(2)]
        hT = cv.get([128, 8, TT], BF16)
        aT = cv.get([128, NF, TT], BF16)
        sq = cv.get([128, 8, 512], BF16)
        rs = [cv.get([128, 512], F32) for _ in range(NSUB)]
        sg = [cv.get([128, 512], F32) for _ in range(2)]
        tmp = sg
        NWI, NWO = 2, 2
        wi_s = [cv.get([128, 8, 256], BF16) for _ in range(NWI)]
        wo_s = [cv.get([128, NF, 128], BF16) for _ in range(NWO)]
        bx = [Buf("xT0"), Buf("xT1")]
        bsq = Buf("sq")
        brs = [Buf() for _ in range(NSUB)]
        bh = [Buf("h%d" % i) for i in range(NSUB)]
        ba = [Buf("a%d" % i) for i in range(NSUB)]
        bsg = [Buf(), Buf()]
        btmp = bsg
        b_wis = [Buf() for _ in range(NWI)]
        b_wos = [Buf() for _ in range(NWO)]
        s_wi = [dsem("wi") for _ in range(NWI)]
        s_wo = [dsem("wo") for _ in range(NWO)]
        s_x = [dsem("x"), dsem("x")]
        s_xst = [dsem("xs"), dsem("xs")]
        stat, bstat = psb[6], b_ps[6]
        wi_v = wi_bf.rearrange("(k p) n -> p k n", p=128)
        wo_v = wo_bf.rearrange("(j p) d -> p j d", p=128)
        n_wi = NT * NF
        n_wo = NT * NK
        st = {"wi": 0, "wo": 0}
        Aof = lambda k: Av[:, l, s_idx, k:k + 1]
        Bof = lambda k: Bv(l, s_idx, k)

        def ld_wi(upto):
            while st["wi"] <= min(upto, n_wi - 1):
                n = st["wi"]
                j = n % NF
                sl_ = n % NWI
                if l == 0 and s_idx == 0:
                    rg, ru = [b_wi0[(j * 128) // 512]], [b_wi0[(DFF + j * 128) // 512]]
                else:
                    rg = ru = [b_wi]
                P.dma("sp", s_wi[sl_], wi_s[sl_][:, :, 0:128], wi_v[:, :, j * 128:(j + 1) * 128], reads=rg, writes=[b_wis[sl_]])
                P.dma("sp", s_wi[sl_], wi_s[sl_][:, :, 128:256], wi_v[:, :, DFF + j * 128:DFF + (j + 1) * 128], reads=ru, wadd=[b_wis[sl_]])
                st["wi"] += 1

        def ld_wo(upto):
            while st["wo"] <= min(upto, n_wo - 1):
                n = st["wo"]
                c = n % NK
                sl_ = n % NWO
                P.dma("sp", s_wo[sl_], wo_s[sl_], wo_v[:, :, c * 128:(c + 1) * 128], reads=[b_wo], writes=[b_wos[sl_]])
                st["wo"] += 1

        def norm_a(t, s_):
            x_, bx_ = xT[t % 2], bx[t % 2]
            sl = slice(s_ * 512, (s_ + 1) * 512)
            act(lambda e: e.activation(sq, x_[:, :, sl], AF.Square), [bx_], writes=[bsq])
            mm_group(stat[:, :], [(onesb[:], sq[:, k, :]) for k in range(8)], [bsq], bstat)
            act(lambda e: e.activation(rs[s_], stat[:, :], AF.Sqrt, bias=eps_t[:, 0:1], scale=1.0 / D), [bstat], writes=[brs[s_]])
            dve(lambda e: e.reciprocal(rs[s_], rs[s_]), [brs[s_]], writes=[brs[s_]])

        def norm_b(t):
            x_, bx_ = xT[t % 2], bx[t % 2]
            for s_ in range(NSUB):
                sl = slice(s_ * 512, (s_ + 1) * 512)
                for k in range(8):
                    kk = k % 2
                    dve(lambda e, k=k, kk=kk, sl=sl, s_=s_: e.scalar_tensor_tensor(tmp[kk], x_[:, k, sl], Aof(k), rs[s_], ALU.mult, ALU.mult),
                        [bx_, brs[s_], b_mod], writes=[btmp[kk]])
                    act(lambda e, k=k, kk=kk, sl=sl: e.activation(hT[:, k, sl], tmp[kk], AF.Identity, bias=Bof(k), scale=1.0),
                        [btmp[kk], b_mod], writes=[bh[s_]] if k == 0 else (), wadd=[bh[s_]] if k else ())

        gi = 0
        load_xT_xs(xT[0], bx[0], 0, s_x[0])
        ld_wi(NWI - 1)
        for s_ in range(NSUB):
            norm_a(0, s_)
        norm_b(0)
        for t in range(NT):
            x_, bx_ = xT[t % 2], bx[t % 2]
            if t + 1 < NT:
                load_xT_xs(xT[(t + 1) % 2], bx[(t + 1) % 2], t + 1, s_x[(t + 1) % 2])
            for j in range(NF):
                n = t * NF + j
                if l == 0 and s_idx == 0:
                    pump([("win", 0), ("wout", 0)], n, NT * NF, [bx_])
                ld_wi(n + NWI - 1)
                if j == NF - 4:
                    ld_wo(t * NK + NWO - 1)
                w = wi_s[n % NWI]
                bw = b_wis[n % NWI]
                for s_ in range(NSUB):
                    sl = slice(s_ * 512, (s_ + 1) * 512)
                    gp, bgp = psb[gi % 2], b_ps[gi % 2]
                    up, bup = psb[2 + gi % 2], b_ps[2 + gi % 2]
                    sgt, bsgt = sg[gi % 2], bsg[gi % 2]
                    gi += 1
                    mm_group(gp[:, :], [(w[:, k, 0:128], hT[:, k, sl]) for k in range(8)], [bw, bh[s_]], bgp)
                    mm_group(up[:, :], [(w[:, k, 128:256], hT[:, k, sl]) for k in range(8)], [bw, bh[s_]], bup)
                    act(lambda e, sgt=sgt, gp=gp: e.activation(sgt, gp[:, :], AF.Silu), [bgp], writes=[bsgt])
                    dve(lambda e, j=j, sl=sl, sgt=sgt, up=up: e.tensor_tensor(aT[:, j, sl], sgt, up[:, :], ALU.mult),
                        [bsgt, bup], writes=[ba[s_]] if j == 0 else (), wadd=[ba[s_]] if j else ())
                if t + 1 < NT:
                    if j == NF - 7:
                        norm_a(t + 1, 0)
                    if j == NF - 3:
                        norm_a(t + 1, 1)
            if t + 1 < NT:
                norm_b(t + 1)
            for c in range(NK):
                n = t * NK + c
                ld_wo(n + NWO - 1)
                w = wo_s[n % NWO]
                bw = b_wos[n % NWO]
                for s_ in range(NSUB):
                    sl = slice(s_ * 512, (s_ + 1) * 512)
                    op_, bop = psb[4 + gi % 2], b_ps[4 + gi % 2]
                    gi += 1
                    mm_group(op_[:, :], [(w[:, j, :], aT[:, j, sl]) for j in range(NF)], [bw, ba[s_]], bop)
                    dve(lambda e, c=c, sl=sl, op_=op_, x_=x_: e.scalar_tensor_tensor(x_[:, c, sl], op_[:, :], Gv[:, l, s_idx, c:c + 1],
                                                                                  x_[:, c, sl], ALU.mult, ALU.add),
                        [bop, b_mod, bx_], wadd=[bx_])
            store_xT_xs(x_, bx_, t, s_xst[t % 2])
        phase_end()

    def phase_mi(l):
        cv.reset()
        xTs = [cv.get([128, 8, TT], F32) for _ in range(2)]
        hTs = [cv.get([128, 8, TT], BF16) for _ in range(2)]
        sq = cv.get([128, 8, 512], BF16)
        rs = cv.get([128, 512], F32)
        tmp = [cv.get([128, 512], F32) for _ in range(2)]
        win = cv.get([128, 8, DIN], BF16)
        tb = [cv.get([128, 512], BF16) for _ in range(2)]
        r1 = [cv.get([128, 512], F32) for _ in range(2)]
        r2 = [cv.get([128, 512], F32) for _ in range(2)]
        qst = [cv.get([128, TT], BF16) for _ in range(2)]
        vst = [cv.get([128, 640], BF16) for _ in range(2)]
        bxs = [Buf(), Buf()]
        bsq, brs, bwin = Buf(), Buf(), Buf()
        bhs = [[Buf() for _ in range(NSUB)] for _ in range(2)]
        btmp = [Buf(), Buf()]
        btb, br1, br2 = [Buf(), Buf()], [Buf(), Buf()], [Buf(), Buf()]
        bqst, bvst = [Buf(), Buf()], [Buf(), Buf()]
        s_xs2 = [dsem("x"), dsem("x")]
        s_w = dsem("w")
        s_q = [dsem("q"), dsem("q")]
        s_v = [dsem("v"), dsem("v")]
        stat, bstat = psb[6], b_ps[6]
        wv = wbf[("win", l)].rearrange("(k p) n -> p k n", p=128)
        bwsrc = b_wbf[("win", l)]
        for c in range(4):
            for hh in range(2):
                h = c + 4 * hh
                P.dma("sp", s_w, win[:, :, c * 128 + hh * 64:c * 128 + hh * 64 + 64], wv[:, :, h * 64:(h + 1) * 64],
                      reads=[bwsrc], wadd=[bwin])
        P.dma("sp", s_w, win[:, :, 512:DIN], wv[:, :, 512:DIN], reads=[bwsrc], wadd=[bwin])
        ccols = [c * 128 for c in range(4)] + [512] + [768 + c * 128 for c in range(4)] + [1280 + c * 128 for c in range(4)]
        gi = 0
        qi = 0
        vi = 0
        load_xT_xs(xTs[0], bxs[0], 0, s_xs2[0])
        for t in range(NT):
            xT, bx = xTs[t % 2], bxs[t % 2]
            if t + 1 < NT:
                load_xT_xs(xTs[(t + 1) % 2], bxs[(t + 1) % 2], t + 1, s_xs2[(t + 1) % 2])
            if l == 0:
                pump([("f2wi", 0), ("f2wo", 0)], t, NT, [bx])
            hT, bh = hTs[t % 2], bhs[t % 2]
            if t == 0:
                norm_h(xT, bx, hT, bh, lambda k: Av[:, l, 1, k:k + 1], lambda k: Bv(l, 1, k),
                       (sq, bsq, rs, brs, tmp, btmp, stat, bstat))
            items = []
            for ch in range(13):
                c0 = ccols[ch]
                q_, bq_ = qst[qi % 2], bqst[qi % 2]
                sq_ = s_q[qi % 2]
                qi += 1
                for s_ in range(NSUB):
                    sl = slice(s_ * 512, (s_ + 1) * 512)
                    tsl = slice(t * TT + s_ * 512, t * TT + (s_ + 1) * 512)
                    tp, btp = psb[gi % 2], b_ps[gi % 2]
                    upp, bupp = psb[2 + gi % 2], b_ps[2 + gi % 2]
                    tb_, btb_ = tb[gi % 2], btb[gi % 2]
                    r1_, br1_ = r1[gi % 2], br1[gi % 2]
                    r2_, br2_ = r2[gi % 2], br2[gi % 2]
                    gi += 1

                    def s1(c0=c0, sl=sl, s_=s_, tp=tp, btp=btp, tb_=tb_, btb_=btb_):
                        mm_group(tp[:, :], [(win[:, k, c0:c0 + 128], hT[:, k, sl]) for k in range(8)], [bwin, bh[s_]], btp)
                        act(lambda e: e.copy(tb_, tp[:, :]), [btp], writes=[btb_])

                    def s2(ch=ch, s_=s_, sl=sl, tsl=tsl, tp=tp, btp=btp, upp=upp, bupp=bupp, tb_=tb_, btb_=btb_,
                           r1_=r1_, br1_=br1_, r2_=r2_, br2_=br2_, q_=q_, bq_=bq_, sq_=sq_, t=t):
                        mm_group(upp[:, :], [(permb[:], tb_)], [btb_, b_const], bupp)
                        dve(lambda e: e.tensor_tensor(r1_, tp[:, :], COS[:, tsl], ALU.mult), [btp, b_rope, btb_], writes=[br1_])
                        dve(lambda e: e.tensor_tensor(r2_, upp[:, :], SIN[:, tsl], ALU.mult), [bupp, b_rope], writes=[br2_])
                        pool(lambda e: e.tensor_tensor(q_[:, sl], r1_, r2_, ALU.add), [br1_, br2_],
                             writes=[bq_] if s_ == 0 else (), wadd=[bq_] if s_ else ())
                        if s_ == NSUB - 1:
                            P.dma("pool", sq_, qk[ch, :, t * TT:(t + 1) * TT], q_, reads=[bq_], wadd=[b_qk])

                    items.append((s1, s2))
            for n in range(len(items) + 1):
                if n < len(items):
                    items[n][0]()
                if n >= 1:
                    items[n - 1][1]()
                if n == 12 and t + 1 < NT:
                    norm_h(xTs[(t + 1) % 2], bxs[(t + 1) % 2], hTs[(t + 1) % 2], bhs[(t + 1) % 2],
                           lambda k: Av[:, l, 1, k:k + 1], lambda k: Bv(l, 1, k),
                           (sq, bsq, rs, brs, tmp, btmp, stat, bstat))
            for blk in range(0 if 'nov' not in DBG else 99, TT // 128):
                s_ = blk // 4
                bsl = slice(blk * 128, (blk + 1) * 128)
                va, bva = psb[4], b_ps[4]
                vb, bvb = psb[5], b_ps[5]
                v_, bv_ = vst[vi % 2], bvst[vi % 2]
                sv_ = s_v[vi % 2]
                vi += 1
                mm_group(va[:, 0:128], [(hT[:, k, bsl], win[:, k, 640:768]) for k in range(8)], [bwin, bh[s_]], bva)
                mm_group(vb[:, :], [(hT[:, k, bsl], win[:, k, 1792:2304]) for k in range(8)], [bwin, bh[s_]], bvb)
                act(lambda e, v_=v_, va=va: e.copy(v_[:, 0:128], va[:, 0:128]), [bva], writes=[bv_])
                dve(lambda e, v_=v_, vb=vb: e.tensor_copy(v_[:, 128:640], vb[:, :]), [bvb], wadd=[bv_])
                tok0 = t * TT + blk * 128
                P.dma("pool", sv_, vs[tok0:tok0 + 128, :], v_, reads=[bv_], wadd=[b_vs])
        phase_end()

    def phase_att(l):
        cv.reset()
        qkT = [cv.get([128, S], BF16) for _ in range(13)]
        NV = 4
        vt = [cv.get([128, 4, 65], BF16) for _ in range(NV)]
        NPT = 6
        pt = [cv.get([128, 384], BF16) for _ in range(NPT)]
        NOS = 3
        ost = [cv.get([128, 260], F32) for _ in range(NOS)]
        bqk = [Buf() for _ in range(13)]
        bvt = [Buf() for _ in range(NV)]
        bpt = [Buf() for _ in range(NPT)]
        bost = [Buf() for _ in range(NOS)]
        s_qk = dsem("qk")
        s_vt = [dsem("vt") for _ in range(NV)]
        s_os = [dsem("os") for _ in range(NOS)]
        for ch in range(13):
            P.dma("sp", s_qk, qkT[ch], qk[ch], reads=[b_qk], writes=[bqk[ch]])
        for i in range(NV):
            pool(lambda e, i=i: e.memset(vt[i][:, :, 64:65], 1.0), [], writes=[bvt[i]])
        cnt = {"v": 0, "p": 0, "o": 0, "st": 0, "acc": 0}
        ST_BANKS = [0, 1, 2]
        ACC_BANKS = [3, 4, 5, 6]
        items = []
        side_ops = []
        if l == 0 and NL > 1:
            a1 = [cv.get([128, 8, 512], BF16) for _ in range(2)]
            ab1 = cv.get([128, 72], F32)
            ba1 = [Buf(), Buf()]
            bab1 = Buf()
            s_a1 = [dsem("a1"), dsem("a1")]
            s_ab1 = dsem("ab1")
            nblk1 = 9 * D // 512
            a1v = adaw_in[1].rearrange("(k p) n -> p k n", p=128)

            def ld_a1(n):
                P.dma("pool", s_a1[n % 2], a1[n % 2], a1v[:, :, n * 512:(n + 1) * 512], writes=[ba1[n % 2]])

            def mk(n):
                def f():
                    if n == 0:
                        P.dma("sp", s_ab1, ab1, adab_in[1], writes=[bab1])
                    if n + 1 < nblk1:
                        ld_a1(n + 1)
                    for jj in range(4):
                        col = n * 4 + jj
                        for k in range(8):
                            first = (n == 0 and jj == 0 and k == 0)
                            mm(psb[7][:, col:col + 1], a1[n % 2][:, k, jj * 128:(jj + 1) * 128], cactb[:, k:k + 1],
                               k == 0, k == 7, [ba1[n % 2], b_cact], writes=[b_ps[7]] if first else (),
                               wadd=() if first else [b_ps[7]], inc=(k == 7 and jj == 3))
                    if n == nblk1 - 1:
                        dve(lambda e: e.tensor_tensor(mod[:, 72:144], psb[7][:, 0:72], ab1, ALU.add), [b_ps[7], bab1, b_mod], wadd=[b_mod])
                        mod_scalars(1)
                return f

            side_ops.append(lambda: ld_a1(0))
            for n in range(nblk1):
                side_ops.append(mk(n))

        def unit(branch, delta, w, qch, kch, pbase, vcol0, nvh, heads_v, ozcol0):
            L = S // delta
            nt = L // 128
            off = w % 128
            maskc0 = 0 if w == 128 else 384
            nh = len(qch)

            def blk_range(m):
                return max(0, 128 * m - off), min(L, 128 * m - off + 128)

            def contrib_tiles(m):
                a, b = blk_range(m)
                return [i for i in range(nt) if (128 * i - w) < b and (128 * i + 128 + w) > a]

            for r in range(delta):
                accs = {}
                for i in range(nt):
                    tile_state = {}
                    qs = max(0, 128 * i - w)
                    qe = min(L, 128 * i + 128 + w)
                    N = qe - qs
                    mc0 = maskc0 + (qs - (128 * i - w))
                    segs = []
                    p_ = qs
                    while p_ < qe:
                        m = (p_ + off) // 128
                        a, b = blk_range(m)
                        e_ = min(qe, b)
                        segs.append((m, p_ - qs, e_ - p_, p_ - a))
                        p_ = e_
                    for hl in range(nh):
                        it = {}

                        def s1(hl=hl, i=i, r=r, tile_state=tile_state, it=it, qs=qs, qe=qe, N=N):
                            if hl == 0:
                                vslot = cnt["v"] % NV
                                cnt["v"] += 1
                                tile_state["v"] = vslot
                                rows = vs[r + delta * 128 * i: r + delta * (128 * i + 127) + 1: delta, vcol0:vcol0 + 64 * nvh]
                                P.dma("sp", s_vt[vslot], vt[vslot][:, 0:nvh, 0:64], rows.rearrange("p (h d) -> p h d", h=nvh),
                                      reads=[b_vs], wadd=[bvt[vslot]])
                            pb = pbase[hl]
                            stb = ST_BANKS[cnt["st"] % len(ST_BANKS)]
                            cnt["st"] += 1
                            it["stb"] = stb
                            kT = qkT[kch[hl]][pb:pb + 64, r + delta * 128 * i: r + delta * (128 * i + 127) + 1: delta]
                            qT = qkT[qch[hl]][pb:pb + 64, r + delta * qs: r + delta * (qe - 1) + 1: delta]
                            mm_group(psb[stb][:, 0:N], [(kT, qT)], [bqk[kch[hl]], bqk[qch[hl]]], b_ps[stb])

                        def s2(it=it, N=N, mc0=mc0):
                            stb = it["stb"]
                            ps_ = cnt["p"] % NPT
                            cnt["p"] += 1
                            it["ps"] = ps_
                            p_sb, bp_ = pt[ps_], bpt[ps_]
                            act(lambda e: e.activation(p_sb[:, 0:N], psb[stb][:, 0:N], AF.Exp, scale=0.125), [b_ps[stb]], writes=[bp_])
                            (pool if (ps_ % 3 == 2) else dve)(lambda e: e.tensor_tensor(p_sb[:, 0:N], p_sb[:, 0:N], cmask[:, mc0:mc0 + N], ALU.mult),
                                                              [bp_, b_const], writes=[bp_])

                        def s3(hl=hl, i=i, r=r, tile_state=tile_state, it=it, segs=segs, accs=accs):
                            p_sb, bp_ = pt[it["ps"]], bpt[it["ps"]]
                            vslot = tile_state["v"]
                            for si, (m, c_, n_, po) in enumerate(segs):
                                if m not in accs:
                                    accs[m] = ACC_BANKS[cnt["acc"] % 4]
                                    cnt["acc"] += 1
                                bank = accs[m]
                                ts = contrib_tiles(m)
                                first = (i == ts[0]) and hl == 0
                                last = (i == ts[-1])
                                o_ap = psb[bank][po:po + n_, hl * 65:(hl + 1) * 65]
                                mm(o_ap, p_sb[:, c_:c_ + n_], vt[vslot][:, heads_v[hl], :], first, last,
                                   [bp_, bvt[vslot]], writes=[b_ps[bank]] if first else (),
                                   wadd=() if first else [b_ps[bank]], inc=(si == len(segs) - 1))
                            if hl != nh - 1:
                                return
                            for m in sorted(list(accs.keys())):
                                ts = contrib_tiles(m)
                                if ts[-1] != i:
                                    continue
                                bank = accs.pop(m)
                                a, b = blk_range(m)
                                n_ = b - a
                                os_ = cnt["o"] % NOS
                                cnt["o"] += 1
                                dve(lambda e, os_=os_, bank=bank, n_=n_: e.tensor_copy(ost[os_][0:n_, 0:nh * 65], psb[bank][0:n_, 0:nh * 65]),
                                    [b_ps[bank]], writes=[bost[os_]])
                                dst = oz[branch, r + delta * a: r + delta * (b - 1) + 1: delta, ozcol0:ozcol0 + nh * 65]
                                P.dma("pool", s_os[os_], dst, ost[os_][0:n_, 0:nh * 65], reads=[bost[os_]], wadd=[b_oz])

                        items.append((s1, s2, s3))

        for g in range(2):
            unit(0, 1, 128, [0, 1, 2, 3], [4, 4, 4, 4], [64 * g] * 4, 64 * g, 1, [0, 0, 0, 0], g * 260)
        for bi, delta in ((1, 1), (2, 4), (3, 16)):
            for hg in range(2):
                qch = [5 + 2 * hg + (hl // 2) for hl in range(4)]
                kch = [9 + 2 * hg + (hl // 2) for hl in range(4)]
                pbs = [64 * (hl % 2) for hl in range(4)]
                unit(bi, delta, 64, qch, kch, pbs, 128 + hg * 256, 4, [0, 1, 2, 3], hg * 260)
        DSK = 5
        every = max(1, (len(items) - 40) // max(1, len(side_ops))) if side_ops else 0
        so = 0
        if NL > 1:
            ckeys = [("f1wi", 1), ("f1wo", 1), ("win", 1), ("wout", 1)] if l == 0 else ([("f2wi", 1), ("f2wo", 1)] if l == 1 else [])
        else:
            ckeys = []
        for n in range(len(items) + DSK):
            if n < len(items):
                items[n][0]()
                items[n][1]()
                if ckeys:
                    pump(ckeys, n, len(items), [bqk[0]])
            if n - DSK >= 0:
                items[n - DSK][2]()
            if side_ops and so < len(side_ops) and n % every == every - 1:
                side_ops[so]()
                so += 1
        while so < len(side_ops):
            side_ops[so]()
            so += 1
        phase_end()

    def phase_cmb(l):
        cv.reset()
        TC = 512
        NG = S // TC
        xT = [cv.get([128, 8, TC], F32) for _ in range(2)]
        wout = cv.get([128, 8, D], BF16)
        yT = [cv.get([128, 8, TC], BF16) for _ in range(2)]
        ozA = [cv.get([128, 4, 520], F32) for _ in range(2)]
        ozB = [cv.get([128, 4, 520], F32) for _ in range(2)]
        ozB4 = cv.get([128, 4, 520], F32)
        ozB16 = cv.get([128, 4, 520], F32)
        on = [cv.get([128, 4, 512], F32) for _ in range(2)]
        y = [cv.get([128, 4, 1024], BF16) for _ in range(2)]
        den = cv.get([128, 2, 4, 8], F32)
        ss = cv.get([128, 2, 4], F32)
        junk = cv.get([128, 4, 512], F32)
        tmpo = [cv.get([128, 512], F32) for _ in range(2)]
        btmpo = [Buf(), Buf()]
        bwout, bden, bss, bjunk = Buf(), Buf(), Buf(), Buf()
        bx = [Buf(), Buf()]
        byT = [Buf(), Buf()]
        by = [[Buf() for _ in range(4)] for _ in range(2)]
        bozA = [Buf(), Buf()]
        bozB = [Buf(), Buf()]
        bozB4, bozB16 = Buf(), Buf()
        bon = [Buf(), Buf()]
        s_w = dsem("w")
        s_x = [dsem("x"), dsem("x")]
        s_xst = [dsem("xs"), dsem("xs")]
        s_ozA = [dsem("oz"), dsem("oz")]
        s_ozB = [dsem("oz"), dsem("oz")]
        s_ozB4, s_ozB16 = dsem("oz"), dsem("oz")
        P.dma("sp", s_w, wout, wbf[("wout", l)].rearrange("(k p) n -> p k n", p=128), reads=[b_wbf[("wout", l)]], writes=[bwout])
        tpsb = psb[7][:, :].bitcast(BF16)
        btps = b_ps[7]
        gbv = gbc[:].rearrange("p (l n) -> p l n", l=NL)
        st = {"gi": 0}

        def load_x(g):
            t = (g * TC) // TT
            P.dma("sp", s_x[g % 2], xT[g % 2], xs_v[:, :, g * TC:(g + 1) * TC], reads=[b_xs[t]], writes=[bx[g % 2]])

        def ozsrc(br, g):
            return oz[br, g * TC:(g + 1) * TC, :].rearrange("(b p) c -> p b c", p=128)

        def load_oz(g):
            P.dma("sp", s_ozA[g % 2], ozA[g % 2], ozsrc(0, g), reads=[b_oz], writes=[bozA[g % 2]])
            P.dma("sp", s_ozB[g % 2], ozB[g % 2], ozsrc(1, g), reads=[b_oz], writes=[bozB[g % 2]])
            P.dma("sp", s_ozB4, ozB4, ozsrc(2, g), reads=[b_oz], writes=[bozB4])
            P.dma("sp", s_ozB16, ozB16, ozsrc(3, g), reads=[b_oz], writes=[bozB16])

        def chain_ops(g):
            oA, bA = ozA[g % 2], bozA[g % 2]
            oB, bB = ozB[g % 2], bozB[g % 2]
            y_, by_ = y[g % 2], by[g % 2]
            oAv = oA.rearrange("p b (h d) -> p b h d", h=8)
            oBv = oB.rearrange("p b (h d) -> p b h d", h=8)
            ops = []
            ops.append(lambda: dve(lambda e: e.tensor_tensor(oB, oB, ozB4, ALU.add), [bB, bozB4], writes=[bB]))
            ops.append(lambda: dve(lambda e: e.tensor_tensor(oB, oB, ozB16, ALU.add), [bB, bozB16], writes=[bB]))
            ops.append(lambda: pool(lambda e: e.memset(ss, 0.0), [], writes=[bss]))
            ops.append(lambda: dve(lambda e: e.tensor_tensor(den[:, 0, :, :], oAv[:, :, :, 64],
                                                             esink[:, l * 8:(l + 1) * 8].unsqueeze(1).broadcast_to([128, 4, 8]), ALU.add),
                                   [bA, b_const], writes=[bden]))
            ops.append(lambda: dve(lambda e: e.tensor_copy(den[:, 1, :, :], oBv[:, :, :, 64]), [bB], wadd=[bden]))
            ops.append(lambda: dve(lambda e: e.reciprocal(den, den), [bden], writes=[bden]))
            for mix, ov, bo in ((0, oAv, bA), (1, oBv, bB)):
                onv = on[mix].rearrange("p b (h d) -> p b h d", h=8)
                ops.append(lambda onv=onv, ov=ov, mix=mix, bo=bo: dve(
                    lambda e: e.tensor_tensor(onv, ov[:, :, :, 0:64], den[:, mix, :, :].unsqueeze(3).broadcast_to([128, 4, 8, 64]), ALU.mult),
                    [bo, bden], writes=[bon[mix]]))

                def sqs(mix=mix):
                    for blk in range(4):
                        act(lambda e, blk=blk: e.activation(junk[:, blk, :], on[mix][:, blk, :], AF.Square, accum_out=ss[:, mix, blk:blk + 1]),
                            [bon[mix]], writes=[bjunk] if blk == 0 else (), wadd=[bss] + ([bjunk] if blk else []))
                ops.append(sqs)
            ops.append(lambda: act(lambda e: e.activation(ss, ss, AF.Sqrt, bias=eps_t[:, 0:1], scale=1.0 / 512), [bss], writes=[bss]))
            ops.append(lambda: dve(lambda e: e.reciprocal(ss, ss), [bss], writes=[bss]))
            for blk in range(4):
                for mix in range(2):
                    ops.append(lambda mix=mix, blk=blk: dve(
                        lambda e: e.scalar_tensor_tensor(y_[:, blk, mix * 512:(mix + 1) * 512], on[mix][:, blk, :], ss[:, mix, blk:blk + 1],
                                                         gbv[:, l, mix * 512:(mix + 1) * 512], ALU.mult, ALU.mult),
                        [bon[mix], bss, b_const], writes=[by_[blk]] if mix == 0 else (), wadd=[by_[blk]] if mix else ()))
                ops.append(lambda blk=blk: transp(g, blk))
            return ops

        def chain(g):
            for op in chain_ops(g):
                op()

        def transp(g, blk):
            y_, by_ = y[g % 2], by[g % 2]
            yT_, byT_ = yT[g % 2], byT[g % 2]
            bank = 6 + (blk % 2)
            tps_ = psb[bank][:, :].bitcast(BF16)
            btp_ = b_ps[bank]
            for fc in range(8):
                P.emit("pe", lambda e, fc=fc: e.transpose(tps_[:, fc * 128:(fc + 1) * 128], y_[:, blk, fc * 128:(fc + 1) * 128], identb[:]),
                       reads=[by_[blk], b_ident], writes=[btp_] if fc == 0 else (), wadd=[btp_] if fc else (), inc=(fc == 7))
            src = tps_.rearrange("p (a b) -> p a b", a=8)
            dst = yT_[:, :, blk * 128:(blk + 1) * 128]
            act(lambda e: e.copy(dst, src), [btp_], writes=[byT_] if blk == 0 else (), wadd=[byT_] if blk else ())

        def wout_chunk(g, c):
            op_, bop = psb[st["gi"] % 2], b_ps[st["gi"] % 2]
            st["gi"] += 1
            x_, bx_ = xT[g % 2], bx[g % 2]
            mm_group(op_[:, :], [(wout[:, fc, c * 128:(c + 1) * 128], yT[g % 2][:, fc, :]) for fc in range(8)], [bwout, byT[g % 2]], bop)
            dve(lambda e: e.scalar_tensor_tensor(x_[:, c, :], op_[:, :], Gv[:, l, 1, c:c + 1], x_[:, c, :], ALU.mult, ALU.add),
                [bop, b_mod, bx_], wadd=[bx_])

        load_x(0)
        load_oz(0)
        chain(0)
        if NG > 1:
            load_oz(1)
        for g in range(NG):
            ops = []
            if g + 1 < NG:
                load_x(g + 1)
                ops = chain_ops(g + 1)
            oi = 0
            for c in range(8):
                wout_chunk(g, c)
                for _ in range(2):
                    if oi < len(ops):
                        ops[oi]()
                        oi += 1
            while oi < len(ops):
                ops[oi]()
                oi += 1
            if g + 2 < NG:
                load_oz(g + 2)
            t = (g * TC) // TT
            first = (g * TC) % TT == 0
            P.dma("pool", s_xst[g % 2], xs_v[:, :, g * TC:(g + 1) * TC], xT[g % 2], reads=[bx[g % 2]],
                  writes=[b_xs[t]] if first else (), wadd=() if first else [b_xs[t]])
        phase_end()

    def phase_final():
        cv.reset()
        TC = 512
        xT = [cv.get([128, 8, TC], F32) for _ in range(2)]
        sq = cv.get([128, 8, 512], BF16)
        rs = cv.get([128, 512], F32)
        yT = cv.get([128, 8, TC], F32)
        ot = [cv.get([128, D], F32) for _ in range(2)]
        bx, bsq, brs, byT = [Buf(), Buf()], Buf(), Buf(), Buf()
        bot = [Buf(), Buf()]
        s_x = [dsem("x"), dsem("x")]
        s_o = [dsem("o"), dsem("o")]
        stat, bstat = psb[6], b_ps[6]
        b_out = Buf()
        ob = 0
        gi = 0
        ntc = S // TC
        P.dma("sp", s_x[0], xT[0], xs_v[:, :, 0:TC], reads=[b_xs[0]], writes=[bx[0]])
        for tc_ in range(ntc):
            if tc_ + 1 < ntc:
                t1 = ((tc_ + 1) * TC) // TT
                P.dma("sp", s_x[(tc_ + 1) % 2], xT[(tc_ + 1) % 2], xs_v[:, :, (tc_ + 1) * TC:(tc_ + 2) * TC], reads=[b_xs[t1]], writes=[bx[(tc_ + 1) % 2]])
            x_, bx_ = xT[tc_ % 2], bx[tc_ % 2]
            act(lambda e, x_=x_: e.activation(sq, x_, AF.Square), [bx_], writes=[bsq])
            mm_group(stat[:, :], [(onesb[:], sq[:, k, :]) for k in range(8)], [bsq], bstat)
            act(lambda e: e.activation(rs, stat[:, :], AF.Sqrt, bias=eps_t[:, 0:1], scale=1.0 / D), [bstat], writes=[brs])
            dve(lambda e: e.reciprocal(rs, rs), [brs], writes=[brs])
            for k in range(8):
                dve(lambda e, k=k, x_=x_: e.scalar_tensor_tensor(yT[:, k, :], x_[:, k, :], gamv[:, 6, k:k + 1], rs, ALU.mult, ALU.mult),
                    [bx_, brs, b_const], writes=[byT] if k == 0 else (), wadd=[byT] if k else ())
            for blk in range(TC // 128):
                o_, bo_ = ot[ob % 2], bot[ob % 2]
                so_ = s_o[ob % 2]
                ob += 1
                for half in range(2):
                    pt_, bpt_ = psb[gi % 4], b_ps[gi % 4]
                    gi += 1
                    for q in range(4):
                        kc = half * 4 + q
                        P.emit("pe", lambda e, pt_=pt_, q=q, kc=kc, blk=blk: e.transpose(pt_[:, q * 128:(q + 1) * 128], yT[:, kc, blk * 128:(blk + 1) * 128], identf[:]),
                               reads=[byT, b_const], writes=[bpt_] if q == 0 else (), wadd=[bpt_] if q else (), inc=(q == 3))
                    if half == 0:
                        act(lambda e, o_=o_, pt_=pt_: e.copy(o_[:, 0:512], pt_[:, :]), [bpt_], writes=[bo_])
                    else:
                        dve(lambda e, o_=o_, pt_=pt_: e.tensor_copy(o_[:, 512:1024], pt_[:, :]), [bpt_], wadd=[bo_])
                tok0 = tc_ * TC + blk * 128
                P.dma("pool", so_, out[tok0:tok0 + 128, :], o_, reads=[bo_], wadd=[b_out])
        phase_end()

    phases = []
    for l in range(NL):
        phases += [("f1_%d" % l, lambda l=l: phase_ffn(l, 0)), ("mi_%d" % l, lambda l=l: phase_mi(l)),
                   ("att_%d" % l, lambda l=l: phase_att(l)), ("cmb_%d" % l, lambda l=l: phase_cmb(l)),
                   ("f2_%d" % l, lambda l=l: phase_ffn(l, 2))]
    phases += [("final", phase_final)]
    for name, fn in phases:
        fn()
        if stop_after == name:
            break
    P.barrier()
    P.finish()
    return nc, P


_CACHE = {}


def _prep_inputs(inputs):
    g = lambda k: np.asarray(inputs[k])
    x, c, positions = g("x"), g("c"), g("positions")
    gam_stack = []
    for l in range(NL):
        gam_stack += [g("norm_ffn1")[l], g("norm_mix")[l], g("norm_ffn2")[l]]
    gam_stack.append(g("final_norm"))
    gam = np.ascontiguousarray(np.stack(gam_stack).reshape(7, 8, 128).transpose(2, 0, 1).reshape(128, 56)).astype(np.float32)
    ada_bT = np.ascontiguousarray(g("ada_b").reshape(NL, 72, 128).transpose(0, 2, 1)).astype(np.float32)
    onorm = np.ascontiguousarray(np.concatenate([g("onorm_a"), g("onorm_b")], axis=1)).astype(np.float32)
    sink = np.ascontiguousarray(g("sink").reshape(1, 16)).astype(np.float32)
    shared = dict(ada_w=np.ascontiguousarray(g("ada_w")), ada_bT=ada_bT, gam=gam, onorm=onorm, sink=sink,
                  ffn1_wi=np.ascontiguousarray(g("ffn1_wi")), ffn1_wo=np.ascontiguousarray(g("ffn1_wo")),
                  w_in=np.ascontiguousarray(g("w_in")), w_out=np.ascontiguousarray(g("w_out")),
                  ffn2_wi=np.ascontiguousarray(g("ffn2_wi")), ffn2_wo=np.ascontiguousarray(g("ffn2_wo")))
    shared.update(_consts())
    maps = []
    for b in range(x.shape[0]):
        m = dict(shared)
        m["x"] = np.ascontiguousarray(x[b])
        m["cT"] = np.ascontiguousarray(c[b].reshape(8, 128).T).astype(np.float32)
        m["pos"] = np.ascontiguousarray(positions[b:b + 1]).astype(np.int32)
        maps.append(m)
    return maps


def kernel(**inputs):
    maps = _prep_inputs(inputs)
    if "nc" not in _CACHE:
        _CACHE["nc"] = build()[0]
    nc = _CACHE["nc"]
    res = run_bass_kernel_spmd(nc, maps, core_ids=list(range(len(maps))))
    return np.stack([np.asarray(r["out"]) for r in res.results], axis=0).astype(np.float32)
```
